# Optimizing a Trainium2 kernel written in Bass

```python
import math
import jax, jax.numpy as jnp
from jax import lax
import numpy as np

D_MODEL = 1024
BATCH = 8
SEQ = 4096
DEPTH = 4

GRID_W = 64
CTX_LEN = 256
MLA_HEADS = 8
MLA_NOPE_DIM = 64
MLA_ROPE_DIM = 32
MLA_V_DIM = 64
MLA_Q_RANK = 256
MLA_KV_RANK = 128
MLA_QK_DIM = MLA_NOPE_DIM + MLA_ROPE_DIM
GDN_HEADS = 4
GDN_HEAD_DIM = 128
GDN_WIDTH = GDN_HEADS * GDN_HEAD_DIM
GDN_CONV = 5
GDN_CHUNK = 64
N_DIR = 2
D_MIX = MLA_HEADS * MLA_V_DIM + GDN_WIDTH
D_FF = 4 * D_MODEL
ROPE_THETA = 10000.0
NORM_EPS = 1e-6
Q_BLOCK = 128
IN_SIZES = (MLA_Q_RANK, MLA_KV_RANK, MLA_ROPE_DIM, 3 * GDN_WIDTH, GDN_WIDTH, N_DIR * GDN_HEADS, N_DIR * GDN_HEADS)
D_IN = MLA_Q_RANK + MLA_KV_RANK + MLA_ROPE_DIM + 4 * GDN_WIDTH + 2 * N_DIR * GDN_HEADS

kernel_name = "hybrid_mla_gdn_prefix_dit"


def _split_points(sizes):
    pts, acc = [], 0
    for s in sizes[:-1]:
        acc += s
        pts.append(acc)
    return pts


def rmsnorm(x, g):
    xf = x.astype(jnp.float32)
    y = xf * lax.rsqrt(jnp.mean(xf * xf, axis=-1, keepdims=True) + NORM_EPS)
    return (y * g.astype(jnp.float32)).astype(x.dtype)


def modulate(x, g, shift, scale):
    return rmsnorm(x, g) * (1 + scale) + shift


def l2norm(x):
    xf = x.astype(jnp.float32)
    return (xf * lax.rsqrt(jnp.sum(xf * xf, axis=-1, keepdims=True) + NORM_EPS)).astype(x.dtype)


def axial_rope_tables(n_tokens, dtype):
    rows = n_tokens // GRID_W
    row = jnp.broadcast_to(jnp.arange(rows)[:, None], (rows, GRID_W)).reshape(-1).astype(jnp.float32)
    col = jnp.broadcast_to(jnp.arange(GRID_W)[None, :], (rows, GRID_W)).reshape(-1).astype(jnp.float32)
    axis_pairs = MLA_ROPE_DIM // 4
    inv_freq = ROPE_THETA ** (-jnp.arange(axis_pairs, dtype=jnp.float32) / axis_pairs)
    ang = jnp.concatenate([row[:, None] * inv_freq, col[:, None] * inv_freq], axis=-1)
    return jnp.cos(ang).astype(dtype), jnp.sin(ang).astype(dtype)


def apply_rope(x, cos, sin):
    half = MLA_ROPE_DIM // 2
    c, s = cos[:, None, :], sin[:, None, :]
    x1, x2 = x[..., :half], x[..., half:]
    return jnp.concatenate([x1 * c - x2 * s, x2 * c + x1 * s], axis=-1)


def centred_conv(x, w):
    pad = GDN_CONV // 2
    return lax.conv_general_dilated(x, w[:, None, :].astype(x.dtype), window_strides=(1,), padding=[(pad, pad)],
                                    dimension_numbers=('NWC', 'WIO', 'NWC'), feature_group_count=x.shape[-1])


def project_tokens(h, w_in, q_a_g, w_q_b, kv_a_g, w_kv_b, conv_w, a_log, dt_bias, rope):
    B, T, _ = h.shape
    p = h @ w_in
    q_a, kv_a, k_r, qkv, z, a, b = jnp.split(p, _split_points(IN_SIZES), axis=-1)
    q = (rmsnorm(q_a, q_a_g) @ w_q_b).reshape(B, T, MLA_HEADS, MLA_QK_DIM)
    kv = (rmsnorm(kv_a, kv_a_g) @ w_kv_b).reshape(B, T, MLA_HEADS, MLA_NOPE_DIM + MLA_V_DIM)
    q_nope, q_pe = q[..., :MLA_NOPE_DIM], q[..., MLA_NOPE_DIM:]
    k_nope, v = kv[..., :MLA_NOPE_DIM], kv[..., MLA_NOPE_DIM:]
    k_pe = k_r[:, :, None, :]
    if rope is not None:
        cos, sin = rope
        q_pe = apply_rope(q_pe, cos, sin)
        k_pe = apply_rope(k_pe, cos, sin)
    q = jnp.concatenate([q_nope, q_pe], axis=-1)
    k = jnp.concatenate([k_nope, jnp.broadcast_to(k_pe, (B, T, MLA_HEADS, MLA_ROPE_DIM))], axis=-1)
    qkv = jax.nn.silu(centred_conv(qkv, conv_w)).reshape(B, T, 3, GDN_HEADS, GDN_HEAD_DIM)
    gq, gk, gv = l2norm(qkv[:, :, 0]), l2norm(qkv[:, :, 1]), qkv[:, :, 2]
    a = a.reshape(B, T, N_DIR, GDN_HEADS).astype(jnp.float32)
    b = b.reshape(B, T, N_DIR, GDN_HEADS).astype(jnp.float32)
    g = -jnp.exp(a_log.astype(jnp.float32)) * jax.nn.softplus(a + dt_bias.astype(jnp.float32))
    beta = jax.nn.sigmoid(b)
    z = z.reshape(B, T, GDN_HEADS, GDN_HEAD_DIM)
    return (q, k, v), (gq, gk, gv, g, beta, z)


def mla_attend(q, k, v):
    B, Tq, H, dqk = q.shape
    nblk = Tq // Q_BLOCK
    scale = dqk ** -0.5
    qb = q.reshape(B, nblk, Q_BLOCK, H, dqk).transpose(1, 0, 2, 3, 4)

    def block(qi):
        s = jnp.einsum('bqhd,bkhd->bhqk', qi, k).astype(jnp.float32) * scale
        p = jax.nn.softmax(s, axis=-1).astype(v.dtype)
        return jnp.einsum('bhqk,bkhd->bqhd', p, v)

    o = lax.map(block, qb)
    return o.transpose(1, 0, 2, 3, 4).reshape(B, Tq, H * v.shape[-1])


def chunk_gated_delta(q, k, v, g, beta, state0):
    f32 = jnp.float32
    B, T, H, DK = q.shape
    DV = v.shape[-1]
    N, C = T // GDN_CHUNK, GDN_CHUNK

    def to_chunks(x):
        return x.astype(f32).reshape(B, N, C, H, x.shape[-1]).transpose(0, 3, 1, 2, 4)

    q = to_chunks(q) * (DK ** -0.5)
    k, v = to_chunks(k), to_chunks(v)
    g = g.astype(f32).reshape(B, N, C, H).transpose(0, 3, 1, 2)
    beta = beta.astype(f32).reshape(B, N, C, H).transpose(0, 3, 1, 2)
    g = jnp.cumsum(g, axis=-1)
    tril = jnp.tril(jnp.ones((C, C), dtype=bool))
    strict = jnp.tril(jnp.ones((C, C), dtype=bool), k=-1)
    diff = g[..., :, None] - g[..., None, :]
    decay = jnp.where(tril, jnp.exp(jnp.where(tril, diff, 0.0)), 0.0)
    k_beta = k * beta[..., None]
    v_beta = v * beta[..., None]
    lower = jnp.where(strict, jnp.einsum('bhnid,bhnjd->bhnij', k_beta, k) * decay, 0.0)
    eye = jnp.eye(C, dtype=f32)
    rhs = jnp.concatenate([v_beta, k_beta * jnp.exp(g)[..., None]], axis=-1)
    sol = lax.linalg.triangular_solve(lower + eye, rhs, left_side=True, lower=True, unit_diagonal=True)
    u, w = sol[..., :DV], sol[..., DV:]
    qk = jnp.einsum('bhnid,bhnjd->bhnij', q, k) * decay
    q_dec = q * jnp.exp(g)[..., None]
    k_tail = k * jnp.exp(g[..., -1:] - g)[..., None]
    g_last = jnp.exp(g[..., -1])

    def step(S, inp):
        qk_n, u_n, w_n, qd_n, kt_n, gl_n = inp
        v_new = u_n - jnp.einsum('bhck,bhkv->bhcv', w_n, S)
        o = jnp.einsum('bhck,bhkv->bhcv', qd_n, S) + jnp.einsum('bhij,bhjv->bhiv', qk_n, v_new)
        S = S * gl_n[..., None, None] + jnp.einsum('bhck,bhcv->bhkv', kt_n, v_new)
        return S, o

    xs = tuple(jnp.moveaxis(t, 2, 0) for t in (qk, u, w, q_dec, k_tail, g_last))
    S_final, o = lax.scan(step, state0, xs)
    return o.transpose(1, 0, 3, 2, 4).reshape(B, T, H, DV), S_final


def gdn_direction(feats, d, state0, reverse):
    gq, gk, gv, g, beta, _ = feats
    g_d, b_d = g[:, :, d], beta[:, :, d]
    if reverse:
        gq, gk, gv, g_d, b_d = (jnp.flip(t, axis=1) for t in (gq, gk, gv, g_d, b_d))
    o, S = chunk_gated_delta(gq, gk, gv, g_d, b_d, state0)
    if reverse:
        o = jnp.flip(o, axis=1)
    return o, S


def gdn_gated_out(o, z, gdn_norm_g):
    B, T = o.shape[:2]
    y = rmsnorm(o, gdn_norm_g) * jax.nn.silu(z.astype(jnp.float32))
    return y.reshape(B, T, GDN_WIDTH).astype(z.dtype)


def gdn_mix(feat_l, feat_c, gdn_norm_g, need_ctx_out):
    B = feat_l[0].shape[0]
    o_l, o_c = 0.0, 0.0
    for d in range(N_DIR):
        zeros = jnp.zeros((B, GDN_HEADS, GDN_HEAD_DIM, GDN_HEAD_DIM), jnp.float32)
        oc, s_ctx = gdn_direction(feat_c, d, zeros, d == 1)
        ol, _ = gdn_direction(feat_l, d, s_ctx, d == 1)
        o_l = o_l + ol
        o_c = o_c + oc
    y_l = gdn_gated_out(o_l, feat_l[5], gdn_norm_g)
    y_c = gdn_gated_out(o_c, feat_c[5], gdn_norm_g) if need_ctx_out else None
    return y_l, y_c


def squared_relu_mlp(h, w_ff1, w_ff2):
    return jnp.square(jax.nn.relu(h @ w_ff1)) @ w_ff2


def layer(x, xc, mod, mod_c, rope, norm1_g, w_in, q_a_g, w_q_b, kv_a_g, w_kv_b, conv_w, a_log, dt_bias,
          gdn_norm_g, w_out, norm2_g, w_ff1, w_ff2, last):
    sh1, sc1, g1, sh2, sc2, g2 = jnp.split(mod, 6, axis=-1)
    csh1, csc1, cg1, csh2, csc2, cg2 = jnp.split(mod_c, 6, axis=-1)
    h = modulate(x, norm1_g, sh1, sc1)
    hc = modulate(xc, norm1_g, csh1, csc1)
    pw = (w_in, q_a_g, w_q_b, kv_a_g, w_kv_b, conv_w, a_log, dt_bias)
    (q, k, v), feat_l = project_tokens(h, *pw, rope)
    (qc, kc, vc), feat_c = project_tokens(hc, *pw, None)
    attn_l = mla_attend(q, jnp.concatenate([kc, k], axis=1), jnp.concatenate([vc, v], axis=1))
    gdn_l, gdn_c = gdn_mix(feat_l, feat_c, gdn_norm_g, not last)
    x = x + g1 * (jnp.concatenate([attn_l, gdn_l], axis=-1) @ w_out)
    x = x + g2 * squared_relu_mlp(modulate(x, norm2_g, sh2, sc2), w_ff1, w_ff2)
    if not last:
        attn_c = mla_attend(qc, kc, vc)
        xc = xc + cg1 * (jnp.concatenate([attn_c, gdn_c], axis=-1) @ w_out)
        xc = xc + cg2 * squared_relu_mlp(modulate(xc, norm2_g, csh2, csc2), w_ff1, w_ff2)
    return x, xc


def setup_inputs(seed: int = 0) -> dict:
    key = jax.random.key(seed)
    ks = jax.random.split(key, 24)
    f32 = jnp.float32
    L = DEPTH

    def nrm(k, shape, scale):
        return jax.random.normal(k, shape, f32) * scale

    def gain(k, shape):
        return 1.0 + 0.02 * jax.random.normal(k, shape, f32)

    dt = jnp.exp(jax.random.uniform(ks[14], (L, N_DIR, GDN_HEADS), f32, math.log(1e-3), math.log(1e-1)))
    return {
        "x": nrm(ks[0], (BATCH, SEQ, D_MODEL), 1.0),
        "c": nrm(ks[1], (BATCH, D_MODEL), 1.0),
        "ctx": nrm(ks[2], (BATCH, CTX_LEN, D_MODEL), 1.0),
        "c_ctx": nrm(ks[3], (D_MODEL,), 1.0),
        "w_ada": nrm(ks[4], (L, D_MODEL, 6 * D_MODEL), 0.5 * D_MODEL ** -0.5),
        "b_ada": nrm(ks[5], (L, 6 * D_MODEL), 0.01),
        "norm1_g": gain(ks[6], (L, D_MODEL)),
        "w_in": nrm(ks[7], (L, D_MODEL, D_IN), D_MODEL ** -0.5),
        "q_a_g": gain(ks[8], (L, MLA_Q_RANK)),
        "w_q_b": nrm(ks[9], (L, MLA_Q_RANK, MLA_HEADS * MLA_QK_DIM), MLA_Q_RANK ** -0.5),
        "kv_a_g": gain(ks[10], (L, MLA_KV_RANK)),
        "w_kv_b": nrm(ks[11], (L, MLA_KV_RANK, MLA_HEADS * (MLA_NOPE_DIM + MLA_V_DIM)), MLA_KV_RANK ** -0.5),
        "conv_w": nrm(ks[12], (L, GDN_CONV, 3 * GDN_WIDTH), GDN_CONV ** -0.5),
        "a_log": jnp.log(jax.random.uniform(ks[13], (L, N_DIR, GDN_HEADS), f32, 1.0, 16.0)),
        "dt_bias": dt + jnp.log(-jnp.expm1(-dt)),
        "gdn_norm_g": gain(ks[15], (L, GDN_HEAD_DIM)),
        "w_out": nrm(ks[16], (L, D_MIX, D_MODEL), D_MIX ** -0.5),
        "norm2_g": gain(ks[17], (L, D_MODEL)),
        "w_ff1": nrm(ks[18], (L, D_MODEL, D_FF), D_MODEL ** -0.5),
        "w_ff2": nrm(ks[19], (L, D_FF, D_MODEL), D_FF ** -0.5),
        "final_norm_g": gain(ks[20], (D_MODEL,)),
    }


def reference(x, c, ctx, c_ctx, w_ada, b_ada, norm1_g, w_in, q_a_g, w_q_b, kv_a_g, w_kv_b, conv_w, a_log,
              dt_bias, gdn_norm_g, w_out, norm2_g, w_ff1, w_ff2, final_norm_g):
    rope = axial_rope_tables(x.shape[1], x.dtype)
    sc = jax.nn.silu(c)
    scc = jax.nn.silu(c_ctx)
    xc = ctx
    for i in range(DEPTH):
        mod = (sc @ w_ada[i] + b_ada[i])[:, None, :]
        mod_c = (scc @ w_ada[i] + b_ada[i])[None, None, :]
        x, xc = layer(x, xc, mod, mod_c, rope, norm1_g[i], w_in[i], q_a_g[i], w_q_b[i], kv_a_g[i], w_kv_b[i],
                      conv_w[i], a_log[i], dt_bias[i], gdn_norm_g[i], w_out[i], norm2_g[i], w_ff1[i], w_ff2[i],
                      i == DEPTH - 1)
    return rmsnorm(x, final_norm_g)
```

```python
import contextlib
import os
import numpy as np
import concourse.bass as bass
import concourse.mybir as mybir
from concourse.bass_utils import run_bass_kernel_spmd

F32 = mybir.dt.float32
BF16 = mybir.dt.bfloat16
AF = mybir.ActivationFunctionType
ALU = mybir.AluOpType

SAME_ENGINE_SYNC = True
N_DMA_SEMS = 24
N_HW_SEMS = 16

T_C, T_L = 256, 4096
T = T_C + T_L
DM = 1024
L = 4
EPS = 1e-6
D_IN = 2480
NCH = T // 64
NCC = T_C // 64


class _Op:
    __slots__ = ("eng", "fn", "deps", "is_dma", "sem_j", "tok", "signal", "idx")


class Prog:
    ENGS = ("pe", "act", "dve", "pool", "sp")

    def __init__(self, nc):
        self.nc = nc
        self.ops = []
        self.state = {}
        self.dma_last = [None] * N_DMA_SEMS
        self.dma_rr = 0
        self.sw_rr = 0
        self.pending_barrier = {}
        self.last_op = {}

    def _sem_key(self, op):
        return ("d", op.sem_j) if op.is_dma else ("e", op.eng)

    def _track(self, op, reads, writes):
        deps = []
        for k in reads:
            st = self.state.get(k)
            if st is not None and st[0] is not None:
                deps.append(st[0])
        for k in writes:
            st = self.state.get(k)
            if st is not None:
                if st[0] is not None:
                    deps.append(st[0])
                deps.extend(st[1].values())
        for k in reads:
            st = self.state.get(k)
            if st is None:
                st = self.state[k] = [None, {}]
            st[1][self._sem_key(op)] = op
        for k in writes:
            self.state[k] = [op, {}]
        pb = self.pending_barrier.pop(op.eng, None)
        if pb:
            deps.extend(pb)
        op.deps = [d for d in deps if d is not op]

    def op(self, eng, fn, r=(), w=()):
        o = _Op()
        o.eng, o.fn, o.is_dma, o.sem_j, o.signal, o.tok = eng, fn, False, None, False, None
        o.idx = len(self.ops)
        self._track(o, r, w)
        self.ops.append(o)
        self.last_op[("e", eng)] = o
        return o

    def dma(self, eng, out, in_, r=(), w=(), **kw):
        o = _Op()
        o.eng, o.is_dma, o.signal, o.tok = eng, True, True, None
        if eng == "pool":
            j = N_HW_SEMS + self.sw_rr
            self.sw_rr = (self.sw_rr + 1) % (N_DMA_SEMS - N_HW_SEMS)
        else:
            j = self.dma_rr
            self.dma_rr = (j + 1) % N_HW_SEMS
        o.sem_j = j
        o.fn = lambda e, out=out, in_=in_, kw=kw: e.dma_start(out=out, in_=in_, **kw)
        o.idx = len(self.ops)
        self._track(o, r, w)
        if self.dma_last[j] is not None:
            o.deps.append(self.dma_last[j])
        self.dma_last[j] = o
        self.ops.append(o)
        self.last_op[("d", j)] = o
        return o

    def barrier(self):
        frontier = list(self.last_op.values())
        for e in self.ENGS:
            self.pending_barrier[e] = list(frontier)

    def finish(self, eng="sp"):
        deps = [o for o in self.dma_last if o is not None]
        o = self.op(eng, lambda e: e.nop(), r=(), w=())
        o.deps.extend(deps)
        return o

    def emit(self, stack):
        nc = self.nc
        for o in self.ops:
            keep = []
            for d in o.deps:
                if not d.is_dma and d.eng == o.eng and not o.is_dma:
                    if d.eng == "pe" or not SAME_ENGINE_SYNC:
                        continue
                d.signal = True
                keep.append(d)
            o.deps = keep
        esem = {e: stack.enter_context(nc.semaphore("s_" + e)) for e in self.ENGS}
        dsem = [stack.enter_context(nc.semaphore("s_dma%d" % j)) for j in range(N_DMA_SEMS)]
        cnt = {e: 0 for e in self.ENGS}
        dcnt = [0] * N_DMA_SEMS
        for o in self.ops:
            if o.is_dma:
                dcnt[o.sem_j] += 16
                o.tok = (dsem[o.sem_j], dcnt[o.sem_j], ("d", o.sem_j))
            elif o.signal:
                cnt[o.eng] += 1
                o.tok = (esem[o.eng], cnt[o.eng], ("e", o.eng))
        by_eng = {e: [o for o in self.ops if o.eng == e] for e in self.ENGS}

        def body(ename):
            def f(eng):
                waited = {}
                for o in by_eng[ename]:
                    need = {}
                    for d in o.deps:
                        sem, val, sk = d.tok
                        if waited.get(sk, 0) < val and need.get(sk, (None, 0))[1] < val:
                            need[sk] = (sem, val)
                    for sk, (sem, val) in need.items():
                        eng.wait_ge(sem, val)
                        waited[sk] = val
                    ins = o.fn(eng)
                    if o.is_dma:
                        ins.then_inc(o.tok[0], 16)
                    elif o.signal:
                        ins.then_inc(o.tok[0], 1)
            return f

        with nc.Block() as block:
            block.tensor(body("pe"))
            block.scalar(body("act"))
            block.vector(body("dve"))
            block.gpsimd(body("pool"))
            block.sync(body("sp"))


class Arena:
    def __init__(self, nc, lo, hi):
        self.nc, self.lo, self.hi, self.cur, self.n = nc, lo, hi, lo, 0

    def alloc(self, shape, dtype, name="t"):
        nbytes = int(np.prod(shape[1:])) * mybir.dt.size(dtype)
        off = (self.cur + 63) // 64 * 64
        assert off + nbytes <= self.hi, ("SBUF overflow", name, off + nbytes, self.hi)
        self.cur = off + nbytes
        self.n += 1
        return self.nc.alloc_sbuf_tensor_at("%s_%d" % (name, self.n), list(shape), dtype, offset=off)

    def mark(self):
        return self.cur

    def reset(self, m):
        self.cur = m


class Builder:
    def __init__(self, debug=False, upto=None, nlayers=L):
        self.debug, self.upto, self.nlayers = debug, upto, nlayers
        nc = self.nc = bass.Bass("TRN2", target_bir_lowering=False)
        self.p = Prog(nc)
        self.uid = 0
        din = lambda n, sh, dt=F32: nc.dram_tensor(n, sh, dt, kind="ExternalInput").ap()
        skind = "ExternalOutput" if debug else "Internal"
        dsc = lambda n, sh, dt: nc.dram_tensor(n, sh, dt, kind=skind).ap()
        self.i = dict(
            x=din("x", [T_L, DM]), ctx=din("ctx", [T_C, DM]), cc=din("cc", [2, DM]),
            w_ada=din("w_ada", [L, DM, 6 * DM]), b_ada=din("b_ada", [L, 6 * DM]),
            norm1_g=din("norm1_g", [L, DM]), w_in=din("w_in", [L, DM, D_IN]),
            q_a_g=din("q_a_g", [L, 256]), w_q_b=din("w_q_b", [L, 256, 768]),
            kv_a_g=din("kv_a_g", [L, 128]), w_kv_b=din("w_kv_b", [L, 128, 1024]),
            conv_w=din("conv_w", [L, 5, 1536]), a_log=din("a_log", [L, 8]), dt_bias=din("dt_bias", [L, 8]),
            gdn_norm_g=din("gdn_norm_g", [L, 128]), w_out=din("w_out", [L, DM, DM]),
            norm2_g=din("norm2_g", [L, DM]), w_ff1=din("w_ff1", [L, DM, 4 * DM]),
            w_ff2=din("w_ff2", [L, 4 * DM, DM]), final_norm_g=din("final_norm_g", [1, DM]),
            c_ident=din("c_ident", [128, 128]), c_ropec=din("c_ropec", [32, T]), c_ropes=din("c_ropes", [32, T]),
            c_gmask=din("c_gmask", [64, 10, 64]),
        )
        self.out = nc.dram_tensor("out", [T_L, DM], F32, kind="ExternalOutput").ap()
        self.xT = dsc("xT", [DM, T], F32)
        self.pqkv = dsc("pqkv", [1536, T], BF16)
        self.zs = dsc("zs", [512, T], BF16)
        self.gb = dsc("gb", [T, 16], F32)
        self.qT = dsc("qT", [8, 96, T], BF16)
        self.kT = dsc("kT", [8, 96, T], BF16)
        self.vtok = dsc("vtok", [T, 8, 64], BF16)
        self.qkvc = dsc("qkvc", [1536, T], F32)
        self.mixT = dsc("mixT", [DM, T], BF16)
        if debug:
            self.dbg_mod = nc.dram_tensor("dbg_mod", [128, L * 96], F32, kind="ExternalOutput").ap()

    def key(self, *a):
        return a

    def bank(self, grp=None):
        if grp is not None:
            lst, i = self.bgrp[grp]
            self.bgrp[grp][1] = (i + 1) % len(lst)
            b = lst[i]
            return self.PS[b], ("ps", b)
        i = self.pi
        self.pi = (i + 1) % len(self.banks)
        b = self.banks[i]
        return self.PS[b], ("ps", b)

    def mm(self, out, lhsT, rhs, start=True, stop=True, r=(), w=()):
        return self.p.op("pe", lambda e: e.matmul(out, lhsT, rhs, start=start, stop=stop), r, w)

    def tr(self, out, in_, ident, r=(), w=()):
        return self.p.op("pe", lambda e: e.transpose(out, in_, ident), r, w)

    def act(self, out, in_, func, scale=1.0, bias=0.0, r=(), w=()):
        return self.p.op("act", lambda e: e.activation(out=out, in_=in_, func=func, bias=bias, scale=scale), r, w)

    def tt(self, eng, out, in0, in1, op, r=(), w=()):
        return self.p.op(eng, lambda e: e.tensor_tensor(out=out, in0=in0, in1=in1, op=op), r, w)

    def ts(self, eng, out, in0, s1, op0, s2=None, op1=None, r=(), w=()):
        if op1 is None:
            return self.p.op(eng, lambda e: e.tensor_scalar(out=out, in0=in0, scalar1=s1, scalar2=None, op0=op0), r, w)
        return self.p.op(eng, lambda e: e.tensor_scalar(out=out, in0=in0, scalar1=s1, scalar2=s2, op0=op0, op1=op1), r, w)

    def stt(self, out, in0, scalar, in1, op0, op1, r=(), w=()):
        return self.p.op("dve", lambda e: e.scalar_tensor_tensor(out=out, in0=in0, scalar=scalar, in1=in1, op0=op0, op1=op1), r, w)

    def cp(self, eng, out, in_, r=(), w=()):
        if eng == "act":
            return self.p.op("act", lambda e: e.copy(out=out, in_=in_), r, w)
        return self.p.op(eng, lambda e: e.tensor_copy(out=out, in_=in_), r, w)

    def rsqrt_from_psum(self, dst, ps, n_div, n, r, w, tmp):
        self.act(tmp, ps, AF.Ln, scale=1.0 / n_div, bias=self.eps_ap, r=list(r) + [("eps",)], w=[("tmp", tmp.tensor.name)])
        self.act(dst, tmp, AF.Exp, scale=-0.5, r=[("tmp", tmp.tensor.name)], w=w)

    def load_rows_T(self, rows_ap, nrows, dst, dkey):
        p = self.p
        self.uid += 1
        k = ("lrt", self.uid)
        tmp = self.lrt_tmp
        p.dma("sp", tmp[0:nrows, :], rows_ap, w=[("lrt_tmp",)])
        ps, pk = self.bank()
        self.tr(ps[:, 0:nrows], tmp[0:nrows, :], self.ident[0:nrows, 0:nrows], r=[("lrt_tmp",), ("ident",)], w=[pk])
        self.cp("dve", dst, ps[:, 0:nrows], r=[pk], w=[dkey])

    def build(self):
        nc, p = self.nc, self.p
        st = contextlib.ExitStack()
        with st:
            self.PS = [st.enter_context(nc.psum_tensor("ps%d" % i, [128, 512], F32)) for i in range(8)]
            self.banks, self.pi = list(range(8)), 0
            self.arena = Arena(nc, (nc.sbuf_base + 63) // 64 * 64, nc.sbuf_top)
            self.prologue()
            for l in range(self.nlayers):
                last = l == L - 1
                if self.upto == "prologue":
                    break
                self.phase1(l)
                if self.upto == "p1":
                    break
                self.phase2(l)
                if self.upto == "p2":
                    break
                self.phase3(l, last)
                if self.upto == "p3":
                    break
                self.phase4(l, last)
                if self.upto in ("p4", "p4s"):
                    break
                self.phase56(l, last)
                if self.upto == "p56":
                    break
            if self.upto is None:
                self.epilogue()
            p.finish("sp")
            p.emit(st)
        return nc

    def prologue(self):
        nc, p, ar, I = self.nc, self.p, self.arena, self.i
        self.ident = ar.alloc([128, 128], F32, "ident")
        p.dma("sp", self.ident[:], I["c_ident"], w=[("ident",)])
        self.ones_f = ar.alloc([128, 128], F32, "ones_f")
        self.ones_bf = ar.alloc([128, 128], BF16, "ones_bf")
        self.eps_t = ar.alloc([128, 1], F32, "eps")
        p.op("dve", lambda e: e.memset(self.ones_f[:], 1.0), w=[("ones_f",)])
        p.op("dve", lambda e: e.memset(self.ones_bf[:], 1.0), w=[("ones_bf",)])
        p.op("dve", lambda e: e.memset(self.eps_t[:], EPS), w=[("eps",)])
        self.eps_ap = self.eps_t[:, 0:1]
        self.lrt_tmp = ar.alloc([128, 128], F32, "lrt_tmp")
        self.n1gT = ar.alloc([128, L, 8], F32, "n1gT")
        self.n2gT = ar.alloc([128, L, 8], F32, "n2gT")
        self.fngT = ar.alloc([128, 8], F32, "fngT")
        self.badaT = ar.alloc([128, L, 48], F32, "badaT")
        self.qagT = ar.alloc([128, L, 2], F32, "qagT")
        self.kvagT = ar.alloc([128, L], F32, "kvagT")
        self.gngT = ar.alloc([128, L], F32, "gngT")
        self.convwT = ar.alloc([128, L, 5, 12], F32, "convwT")
        self.ccT = ar.alloc([128, 2, 8], F32, "ccT")
        self.scT = ar.alloc([128, 2, 8], F32, "scT")
        self.modT = ar.alloc([128, L, 48, 2], F32, "modT")
        self.A1 = ar.alloc([128, L, 8, 2], F32, "A1")
        self.A2 = ar.alloc([128, L, 8, 2], F32, "A2")
        self.alog = ar.alloc([128, L * 8], F32, "alog")
        self.negA = ar.alloc([128, L * 8], F32, "negA")
        self.dtb = ar.alloc([128, L * 8], F32, "dtb")
        fl = lambda t: t[:].rearrange("p a b -> p (a b)") if len(t.shape) == 3 else t[:]
        self.load_rows_T(I["norm1_g"].rearrange("l (c q) -> (l c) q", q=128), 32, fl(self.n1gT), ("n1gT",))
        self.load_rows_T(I["norm2_g"].rearrange("l (c q) -> (l c) q", q=128), 32, fl(self.n2gT), ("n2gT",))
        self.load_rows_T(I["final_norm_g"].rearrange("l (c q) -> (l c) q", q=128), 8, self.fngT[:], ("fngT",))
        bview = I["b_ada"].rearrange("l (f q) -> (l f) q", q=128)
        bflat = fl(self.badaT)
        for h in range(2):
            self.load_rows_T(bview[h * 96:(h + 1) * 96, :], 96, bflat[:, h * 96:(h + 1) * 96], ("badaT", h))
        self.load_rows_T(I["q_a_g"].rearrange("l (c q) -> (l c) q", q=128), 8, fl(self.qagT), ("qagT",))
        self.load_rows_T(I["kv_a_g"], 4, self.kvagT[:], ("kvagT",))
        self.load_rows_T(I["gdn_norm_g"], 4, self.gngT[:], ("gngT",))
        cview = I["conv_w"].rearrange("l j (c q) -> (l j c) q", q=128)
        cflat = self.convwT[:].rearrange("p l j c -> p (l j c)")
        for h in range(2):
            self.load_rows_T(cview[h * 120:(h + 1) * 120, :], 120, cflat[:, h * 120:(h + 1) * 120], ("convwT", h))
        self.load_rows_T(I["cc"].rearrange("k (c q) -> (k c) q", q=128), 16, fl(self.ccT), ("ccT",))
        self.act(fl(self.scT), fl(self.ccT), AF.Silu, r=[("ccT",)], w=[("scT",)])
        p.dma("sp", self.alog[:], I["a_log"].rearrange("l e -> (l e)").partition_broadcast(128), w=[("alog",)])
        p.dma("sp", self.dtb[:], I["dt_bias"].rearrange("l e -> (l e)").partition_broadcast(128), w=[("dtb",)])
        self.act(self.negA[:], self.alog[:], AF.Exp, r=[("alog",)], w=[("negA",)])
        self.ts("dve", self.negA[:], self.negA[:], -1.0, ALU.mult, r=[("negA",)], w=[("negA",)])

        m0 = ar.mark()
        XL = [ar.alloc([128, DM], F32, "XL") for _ in range(2)]
        XO = [ar.alloc([128, 8, 128], F32, "XO") for _ in range(2)]
        xTv = self.xT.rearrange("(c q) t -> q c t", q=128)
        for i in range(T // 128):
            b = i % 2
            src = I["ctx"][i * 128:(i + 1) * 128, :] if i < 2 else I["x"][(i - 2) * 128:(i - 1) * 128, :]
            p.dma("sp", XL[b][:], src, w=[("XL", b)])
            for hb in range(2):
                ps, pk = self.bank()
                for j in range(4):
                    c = hb * 4 + j
                    self.tr(ps[:, j * 128:(j + 1) * 128], XL[b][:, c * 128:(c + 1) * 128], self.ident[:],
                            r=[("XL", b), ("ident",)], w=[pk])
                self.cp("dve" if hb == 0 else "act", XO[b][:, hb * 4:(hb + 1) * 4, :],
                        ps[:].rearrange("q (j t) -> q j t", j=4), r=[pk], w=[("XO", b, hb)])
            p.dma("sp", xTv[:, :, i * 128:(i + 1) * 128], XO[b][:], r=[("XO", b, 0), ("XO", b, 1)], w=[("xT", i)])
        WA = [ar.alloc([128, 8, 512], F32, "WA") for _ in range(2)]
        nb = 0
        for l in range(self.nlayers):
            ps, pk = self.bank()
            for fb in range(12):
                b = nb % 2
                nb += 1
                p.dma("sp" if fb % 2 == 0 else "act", WA[b][:],
                      I["w_ada"][l, :, fb * 512:(fb + 1) * 512].rearrange("(k q) f -> q k f", q=128), w=[("WA", b)])
                for j in range(4):
                    fc = fb * 4 + j
                    for k in range(8):
                        self.mm(ps[:, fc * 2:fc * 2 + 2], WA[b][:, k, j * 128:(j + 1) * 128], self.scT[:, :, k],
                                start=(k == 0), stop=(k == 7), r=[("WA", b), ("scT",)], w=[pk])
            self.tt("dve", self.modT[:, l, :, :], ps[:, 0:96].rearrange("q (f k) -> q f k", k=2),
                    self.badaT[:, l, :].unsqueeze(2).broadcast_to([128, 48, 2]), ALU.add,
                    r=[pk, ("badaT", 0), ("badaT", 1)], w=[("modT", l)])
            for (A, g, o) in ((self.A1, self.n1gT, 8), (self.A2, self.n2gT, 32)):
                self.stt(A[:, l, :, :], self.modT[:, l, o:o + 8, :], 1.0,
                         g[:, l, :].unsqueeze(2).broadcast_to([128, 8, 2]), ALU.add, ALU.mult,
                         r=[("modT", l), ("n1gT",), ("n2gT",)], w=[("A", l, o)])
        if self.debug:
            p.dma("sp", self.dbg_mod[:, 0:self.nlayers * 96],
                  self.modT[:, 0:self.nlayers, :, :].rearrange("q l f k -> q (l f k)"),
                  r=[("modT", l) for l in range(self.nlayers)])
        ar.reset(m0)
        self.m_phase = m0
        p.barrier()

    def modv(self, l, which, c, col):
        return self.modT[:, l, which * 8 + c, col:col + 1]

    def load_weight_bf16(self, dst_ap, src_ap, wkey, stg):
        p = self.p
        K_, F_ = dst_ap.shape[1], dst_ap.shape[2]
        for k in range(K_):
            for f0 in range(0, F_, 2048):
                fw = min(2048, F_ - f0)
                i = self.stg_i
                self.stg_i += 1
                sb = i % len(stg)
                p.dma("sp" if i % 2 == 0 else "act", stg[sb][:, 0:fw], src_ap[:, k, f0:f0 + fw], w=[("stg", sb)])
                eng = ("dve", "act")[i % 2]
                self.cp(eng, dst_ap[:, k, f0:f0 + fw], stg[sb][:, 0:fw], r=[("stg", sb)], w=[(wkey[0], i)])
                self.wkeys.setdefault(wkey[0], []).append((wkey[0], i))

    def phase1(self, l):
        nc, p, ar, I = self.nc, self.p, self.arena, self.i
        ar.reset(self.m_phase)
        self.banks, self.pi = list(range(8)), 0
        Win = ar.alloc([128, 8, D_IN], BF16, "Win")
        WinR = ar.alloc([128, 8, 32], BF16, "WinR")
        Wqb = ar.alloc([128, 2, 8, 96], BF16, "Wqb")
        WqbR = ar.alloc([128, 2, 8, 32], BF16, "WqbR")
        Wkvb = ar.alloc([128, 8, 128], BF16, "Wkvb")
        mW = ar.mark()
        stg = [ar.alloc([128, 2048], F32, "stg") for _ in range(3)]
        self.wkeys, self.stg_i = {}, 0
        self.load_weight_bf16(Win[:], I["w_in"][l].rearrange("(k q) f -> q k f", q=128), ("Win",), stg)
        self.load_weight_bf16(Wqb[:].rearrange("q k h e -> q k (h e)"), I["w_q_b"][l].rearrange("(k q) f -> q k f", q=128), ("Wqb",), stg)
        self.load_weight_bf16(Wkvb[:].rearrange("q h e -> q (h e)").unsqueeze(1), I["w_kv_b"][l].unsqueeze(1), ("Wkvb",), stg)
        p.barrier()
        ar.reset(mW)
        self.ts("dve", WinR[:, :, 0:16], Win[:, :, 400:416], -1.0, ALU.mult, r=[("Win",)], w=[("WinR", 0)])
        self.cp("dve", WinR[:, :, 16:32], Win[:, :, 384:400], r=[("Win",)], w=[("WinR", 1)])
        self.ts("dve", WqbR[:, :, :, 0:16], Wqb[:, :, :, 80:96], -1.0, ALU.mult, r=[("Wqb",)], w=[("WqbR", 0)])
        self.cp("dve", WqbR[:, :, :, 16:32], Wqb[:, :, :, 64:80], r=[("Wqb",)], w=[("WqbR", 1)])
        rWinR = [("WinR", 0), ("WinR", 1)]
        rWqbR = [("WqbR", 0), ("WqbR", 1)]

        STOP = int(os.environ.get("P1_STOP", "99"))
        if STOP == 0:
            p.barrier()
            return
        X = [ar.alloc([128, 8, 512], F32, "X") for _ in range(2)]
        SQ = ar.alloc([128, 8, 512], BF16, "SQ")
        HT = ar.alloc([128, 8, 512], BF16, "HT")
        T1 = [ar.alloc([128, 512], F32, "T1") for _ in range(2)]
        LN = ar.alloc([128, 512], F32, "LN")
        RS = ar.alloc([128, 512], F32, "RS")
        RSQ = ar.alloc([128, 512], F32, "RSQ")
        QA = ar.alloc([128, 3, 512], F32, "QA")
        SQ2 = ar.alloc([128, 3, 512], BF16, "SQ2")
        QAN = ar.alloc([128, 3, 512], BF16, "QAN")
        PQ = ar.alloc([128, 12, 512], BF16, "PQ")
        ZSt = ar.alloc([128, 4, 512], BF16, "ZSt")
        ZE = [ar.alloc([128, 512], F32, "ZE") for _ in range(2)]
        KTt = ar.alloc([128, 8, 512], BF16, "KTt")
        QTt = ar.alloc([128, 8, 512], BF16, "QTt")
        VT = ar.alloc([128, 4, 512], BF16, "VT")
        CT = ar.alloc([128, 512], F32, "CT")
        STb = ar.alloc([128, 512], F32, "STb")
        R1 = [ar.alloc([128, 512], F32, "R1") for _ in range(2)]
        R2 = [ar.alloc([128, 512], F32, "R2") for _ in range(2)]
        KPE = ar.alloc([128, 512], BF16, "KPE")
        GBt = ar.alloc([128, 4, 16], F32, "GBt")
        GX = ar.alloc([128, 4, 16], F32, "GX")

        xTv = self.xT.rearrange("(c q) t -> q c t", q=128)
        tiles = [(0, 256, 0)] + [(256 + 512 * i, 512, 1) for i in range(8)]
        ones_bf = self.ones_bf
        for ti, (t0, n, col) in enumerate(tiles):
            xb = ti % 2
            Xt = X[xb]
            p.dma("sp", Xt[:, :, 0:n], xTv[:, :, t0:t0 + n], r=[("xT_l", l - 1)], w=[("X", xb)])
            p.dma("act", CT[64:96, 0:n], I["c_ropec"][:, t0:t0 + n], w=[("CT",)])
            p.dma("act", STb[64:96, 0:n], I["c_ropes"][:, t0:t0 + n], w=[("STb",)])
            self.tt("dve", SQ[:, :, 0:n], Xt[:, :, 0:n], Xt[:, :, 0:n], ALU.mult, r=[("X", xb)], w=[("SQ",)])
            ps, pk = self.bank()
            for k in range(8):
                self.mm(ps[:, 0:n], ones_bf[:], SQ[:, k, 0:n], start=(k == 0), stop=(k == 7), r=[("SQ",), ("ones_bf",)], w=[pk])
            self.rsqrt_from_psum(RS[:, 0:n], ps[:, 0:n], float(DM), n, r=[pk], w=[("RS",)], tmp=LN[:, 0:n])
            for k in range(8):
                tb = k % 2
                self.stt(T1[tb][:, 0:n], Xt[:, k, 0:n], self.A1[:, l, k, col:col + 1], RS[:, 0:n], ALU.mult, ALU.mult,
                         r=[("X", xb), ("RS",), ("A", l, 8)], w=[("T1", tb)])
                self.act(HT[:, k, 0:n], T1[tb][:, 0:n], AF.Identity, bias=self.modv(l, 0, k, col),
                         r=[("T1", tb), ("modT", l)], w=[("HT", k)])
            rHT = [("HT", k) for k in range(8)]
            if STOP == 1:
                continue

            def proj(c0, m, out_ap, lhs=None, rot=False):
                for k in range(8):
                    lt = WinR[:, k, :] if rot else Win[:, k, c0:c0 + m]
                    self.mm(out_ap, lt, HT[:, k, 0:n], start=(k == 0), stop=(k == 7),
                            r=rHT + [("Win",)] + (rWinR if rot else []), w=[out_ap_key[0]])

            for j in range(3):
                ps, pk = self.bank()
                out_ap_key = [pk]
                proj(j * 128, 128, ps[:, 0:n])
                self.cp("dve", QA[:, j, 0:n], ps[:, 0:n], r=[pk], w=[("QA", j)])
                self.tt("dve", SQ2[:, j, 0:n], QA[:, j, 0:n], QA[:, j, 0:n], ALU.mult, r=[("QA", j)], w=[("SQ2", j)])
            ps, pk = self.bank()
            for j in range(2):
                self.mm(ps[:, 0:n], ones_bf[:], SQ2[:, j, 0:n], start=(j == 0), stop=(j == 1), r=[("SQ2", j), ("ones_bf",)], w=[pk])
            self.rsqrt_from_psum(RSQ[:, 0:n], ps[:, 0:n], 256.0, n, r=[pk], w=[("RSQ",)], tmp=LN[:, 0:n])
            for j in range(2):
                self.stt(QAN[:, j, 0:n], QA[:, j, 0:n], self.qagT[:, l, j:j + 1], RSQ[:, 0:n], ALU.mult, ALU.mult,
                         r=[("QA", j), ("RSQ",), ("qagT",)], w=[("QAN", j)])
            ps, pk = self.bank()
            self.mm(ps[:, 0:n], ones_bf[:], SQ2[:, 2, 0:n], r=[("SQ2", 2), ("ones_bf",)], w=[pk])
            self.rsqrt_from_psum(RSQ[:, 0:n], ps[:, 0:n], 128.0, n, r=[pk], w=[("RSQ",)], tmp=LN[:, 0:n])
            self.stt(QAN[:, 2, 0:n], QA[:, 2, 0:n], self.kvagT[:, l:l + 1], RSQ[:, 0:n], ALU.mult, ALU.mult,
                     r=[("QA", 2), ("RSQ",), ("kvagT",)], w=[("QAN", 2)])
            if STOP == 2:
                continue
            psA, pkA = self.bank()
            out_ap_key = [pkA]
            proj(384, 32, psA[64:96, 0:n])
            psB, pkB = self.bank()
            out_ap_key = [pkB]
            proj(0, 32, psB[64:96, 0:n], rot=True)
            self.tt("dve", R1[0][64:96, 0:n], psA[64:96, 0:n], CT[64:96, 0:n], ALU.mult, r=[pkA, ("CT",)], w=[("R1", 0)])
            self.tt("dve", R2[0][64:96, 0:n], psB[64:96, 0:n], STb[64:96, 0:n], ALU.mult, r=[pkB, ("STb",)], w=[("R2", 0)])
            self.tt("dve", KPE[64:96, 0:n], R1[0][64:96, 0:n], R2[0][64:96, 0:n], ALU.add, r=[("R1", 0), ("R2", 0)], w=[("KPE",)])
            self.cp("act", KTt[64:96, :, 0:n], KPE[64:96, 0:n].unsqueeze(1).broadcast_to([32, 8, n]),
                    r=[("KPE",)], w=[("KTt", "pe")])
            if STOP == 3:
                continue
            for j in range(12):
                ps, pk = self.bank()
                out_ap_key = [pk]
                proj(416 + 128 * j, 128, ps[:, 0:n])
                self.cp("act" if j % 2 == 0 else "dve", PQ[:, j, 0:n], ps[:, 0:n], r=[pk], w=[("PQ", j)])
            for hh in range(2):
                p.dma("sp", self.pqkv.rearrange("(c q) t -> q c t", q=128)[:, hh * 6:(hh + 1) * 6, t0:t0 + n], PQ[:, hh * 6:(hh + 1) * 6, 0:n],
                      r=[("PQ", j) for j in range(hh * 6, (hh + 1) * 6)], w=[("pqkv", ti, hh)])
            for j in range(4):
                ps, pk = self.bank()
                out_ap_key = [pk]
                proj(1952 + 128 * j, 128, ps[:, 0:n])
                self.act(ZSt[:, j, 0:n], ps[:, 0:n], AF.Silu, r=[pk], w=[("ZSt", j)])
            p.dma("sp", self.zs.rearrange("(c q) t -> q c t", q=128)[:, :, t0:t0 + n], ZSt[:, :, 0:n],
                  r=[("ZSt", j) for j in range(4)], w=[("zs", ti)])
            if STOP == 4:
                continue
            nj = n // 128
            for j in range(nj):
                ps, pk = self.bank()
                for k in range(8):
                    self.mm(ps[:, 0:16], HT[:, k, j * 128:(j + 1) * 128], Win[:, k, 2464:2480], start=(k == 0), stop=(k == 7),
                            r=rHT + [("Win",)], w=[pk])
                self.tt("dve", GX[:, j, 0:8], ps[:, 0:8], self.dtb[:, l * 8:(l + 1) * 8], ALU.add, r=[pk, ("dtb",)], w=[("GX", j)])
                self.act(GX[:, j, 0:8], GX[:, j, 0:8], AF.Exp, r=[("GX", j)], w=[("GX", j)])
                self.cp("dve", GX[:, j, 8:16], ps[:, 8:16], r=[pk, ("GX", j)], w=[("GX", j)])
                self.act(GX[:, j, 8:16], GX[:, j, 8:16], AF.Exp, scale=-1.0, r=[("GX", j)], w=[("GX", j)])
                self.act(GX[:, j, 0:8], GX[:, j, 0:8], AF.Ln, bias=1.0, r=[("GX", j)], w=[("GX", j)])
                self.tt("dve", GBt[:, j, 0:8], GX[:, j, 0:8], self.negA[:, l * 8:(l + 1) * 8], ALU.mult, r=[("GX", j), ("negA",)], w=[("GBt", j)])
                self.ts("dve", GX[:, j, 8:16], GX[:, j, 8:16], 1.0, ALU.add, r=[("GX", j)], w=[("GX", j)])
                p.op("dve", lambda e, o=GBt[:, j, 8:16], i_=GX[:, j, 8:16]: e.reciprocal(out=o, in_=i_), r=[("GX", j), ("GBt", j)], w=[("GBt", j)])
            p.dma("sp", self.gb[t0:t0 + n, :].rearrange("(j q) f -> q j f", q=128), GBt[:, 0:nj, :],
                  r=[("GBt", j) for j in range(nj)], w=[("gb", ti)])
            if STOP == 5:
                continue
            for h in range(8):
                psA, pkA = self.bank()
                for kk in range(2):
                    self.mm(psA[0:96, 0:n], Wqb[:, kk, h, :], QAN[:, kk, 0:n], start=(kk == 0), stop=(kk == 1),
                            r=[("Wqb",), ("QAN", 0), ("QAN", 1)], w=[pkA])
                psB, pkB = self.bank()
                for kk in range(2):
                    self.mm(psB[64:96, 0:n], WqbR[:, kk, h, :], QAN[:, kk, 0:n], start=(kk == 0), stop=(kk == 1),
                            r=rWqbR + [("QAN", 0), ("QAN", 1)], w=[pkB])
                rb = h % 2
                self.cp("dve", QTt[0:64, h, 0:n], psA[0:64, 0:n], r=[pkA], w=[("QTt", h, 0)])
                self.tt("dve", R1[rb][64:96, 0:n], psA[64:96, 0:n], CT[64:96, 0:n], ALU.mult, r=[pkA, ("CT",)], w=[("R1", rb)])
                self.tt("dve", R2[rb][64:96, 0:n], psB[64:96, 0:n], STb[64:96, 0:n], ALU.mult, r=[pkB, ("STb",)], w=[("R2", rb)])
                self.tt("dve", QTt[64:96, h, 0:n], R1[rb][64:96, 0:n], R2[rb][64:96, 0:n], ALU.add,
                        r=[("R1", rb), ("R2", rb)], w=[("QTt", h, 1)])
            p.dma("sp", self.qT[:, :, t0:t0 + n].rearrange("h d t -> d h t"), QTt[0:96, :, 0:n],
                  r=[("QTt", h, s) for h in range(8) for s in range(2)], w=[("qT", ti)])
            if STOP == 6:
                continue
            for h in range(8):
                ps, pk = self.bank()
                self.mm(ps[0:64, 0:n], Wkvb[:, h, 0:64], QAN[:, 2, 0:n], r=[("Wkvb",), ("QAN", 2)], w=[pk])
                self.cp("act" if h % 2 == 0 else "dve", KTt[0:64, h, 0:n], ps[0:64, 0:n], r=[pk], w=[("KTt", h)])
            p.dma("sp", self.kT[:, :, t0:t0 + n].rearrange("h d t -> d h t"), KTt[0:96, :, 0:n],
                  r=[("KTt", h) for h in range(8)] + [("KTt", "pe")], w=[("kT", ti)])
            for j in range(nj):
                ps, pk = self.bank()
                self.mm(ps[:, :].rearrange("q (h e) -> q h e", e=64), QAN[:, 2, j * 128:(j + 1) * 128], Wkvb[:, :, 64:128],
                        r=[("Wkvb",), ("QAN", 2)], w=[pk])
                self.cp("act" if j % 2 == 0 else "dve", VT[:, j, :], ps[:, :], r=[pk], w=[("VT", j)])
            p.dma("sp", self.vtok[t0:t0 + n].rearrange("(j q) h e -> q j (h e)", q=128), VT[:, 0:nj, :],
                  r=[("VT", j) for j in range(nj)], w=[("vtok", ti)])
        p.barrier()

    def phase2(self, l):
        nc, p, ar, I = self.nc, self.p, self.arena, self.i
        ar.reset(self.m_phase)
        self.banks, self.pi = list(range(8)), 0
        Dg = ar.alloc([128, 12, 5, 128], BF16, "Dg")
        for c in range(12):
            for j in range(5):
                self.ts("dve", Dg[:, c, j, :], self.ident[:], self.convwT[:, l, j, c:c + 1], ALU.mult,
                        r=[("ident",), ("convwT", 0), ("convwT", 1)], w=[("Dg", c, j)])
        PT = [ar.alloc([128, 12, 516], BF16, "PT") for _ in range(2)]
        QC = [ar.alloc([128, 12, 512], F32, "QC") for _ in range(2)]
        SQ = [ar.alloc([128, 512], BF16, "SQ") for _ in range(2)]
        LN8 = ar.alloc([128, 8, 512], F32, "LN8")
        pv = self.pqkv.rearrange("(c q) t -> q c t", q=128)
        qv = self.qkvc.rearrange("(c q) t -> q c t", q=128)
        tiles = [(0, 256, 0, T_C)] + [(256 + 512 * i, 512, T_C, T) for i in range(8)]
        for ti, (t0, n, s0, s1) in enumerate(tiles):
            b = ti % 2
            lo, hi = t0 - 2, t0 + n + 2
            vlo, vhi = max(lo, s0), min(hi, s1)
            if lo < s0:
                p.op("dve", lambda e, a=PT[b][:, :, 0:2]: e.memset(a, 0.0), w=[("PT", b, 0), ("PT", b, 1)])
            if hi > s1:
                p.op("dve", lambda e, a=PT[b][:, :, n + 2:n + 4]: e.memset(a, 0.0), w=[("PT", b, 0), ("PT", b, 1)])
            for hh in range(2):
                p.dma("sp" if hh == 0 else "act", PT[b][:, hh * 6:(hh + 1) * 6, vlo - lo:vhi - lo], pv[:, hh * 6:(hh + 1) * 6, vlo:vhi],
                      r=[("PT", b, hh)], w=[("PT", b, hh)])
            for c in range(12):
                ps, pk = self.bank()
                for j in range(5):
                    self.mm(ps[:, 0:n], Dg[:, c, j, :], PT[b][:, c, j:j + n], start=(j == 0), stop=(j == 4),
                            r=[("PT", b, c // 6), ("Dg", c, j)], w=[pk])
                self.act(QC[b][:, c, 0:n], ps[:, 0:n], AF.Silu, r=[pk], w=[("QC", b, c)])
            for c in range(8):
                eb = c % 2
                self.tt("dve", SQ[eb][:, 0:n], QC[b][:, c, 0:n], QC[b][:, c, 0:n], ALU.mult, r=[("QC", b, c)], w=[("SQ", eb)])
                ps2, pk2 = self.bank()
                self.mm(ps2[:, 0:n], self.ones_bf[:], SQ[eb][:, 0:n], r=[("SQ", eb), ("ones_bf",)], w=[pk2])
                self.act(LN8[:, c, 0:n], ps2[:, 0:n], AF.Ln, bias=self.eps_ap, r=[pk2, ("eps",)], w=[("LN8", c)])
            self.act(LN8[:, :, 0:n], LN8[:, :, 0:n], AF.Exp, scale=-0.5, r=[("LN8", c) for c in range(8)], w=[("LN8", c) for c in range(8)])
            for c in range(8):
                self.stt(QC[b][:, c, 0:n], QC[b][:, c, 0:n], (128.0 ** -0.5) if c < 4 else 1.0, LN8[:, c, 0:n], ALU.mult, ALU.mult,
                         r=[("QC", b, c), ("LN8", c)], w=[("QC", b, c)])
            for hh in range(2):
                p.dma("sp", qv[:, hh * 6:(hh + 1) * 6, t0:t0 + n], QC[b][:, hh * 6:(hh + 1) * 6, 0:n],
                      r=[("QC", b, c) for c in range(hh * 6, (hh + 1) * 6)], w=[("qkvc", ti, hh)])
        p.barrier()

    def phase3(self, l, last):
        nc, p, ar, I = self.nc, self.p, self.arena, self.i
        ar.reset(self.m_phase)
        KTh = [ar.alloc([128, T], BF16, "KTh") for _ in range(2)]
        QTh = [ar.alloc([128, T], BF16, "QTh") for _ in range(2)]
        NKT = T // 128
        Vall = ar.alloc([128, NKT, 512], BF16, "Vall")
        Vaug = ar.alloc([128, NKT, 8, 128], BF16, "Vaug")
        PTt = [ar.alloc([128, 512], BF16, "PTt") for _ in range(4)]
        Rr = [ar.alloc([128, 512], F32, "Rr") for _ in range(2)]
        AT = [ar.alloc([128, 512], BF16, "AT") for _ in range(2)]
        vv = self.vtok.rearrange("(j q) h e -> q j (h e)", q=128)
        for g_ in range(0, NKT, 4):
            ge = min(NKT, g_ + 4)
            p.dma("sp" if (g_ // 4) % 2 == 0 else "act", Vall[:, g_:ge, :], vv[:, g_:ge, :], w=[("Vall", g_)])
            p.op("dve", lambda e, a=Vaug[:, g_:ge, :, 64:128]: e.memset(a, 1.0), w=[("Vaug1", g_)])
            self.cp("act" if (g_ // 4) % 2 == 0 else "dve", Vaug[:, g_:ge, :, 0:64], Vall[:, g_:ge, :].rearrange("q j (h e) -> q j h e", e=64),
                    r=[("Vall", g_)], w=[("Vaug0", g_)])
        rV = [("Vaug0", g_) for g_ in range(0, NKT, 4)] + [("Vaug1", g_) for g_ in range(0, NKT, 4)]
        scale = 96.0 ** -0.5
        qtiles = ([] if last else [(0, 256, 2)]) + [(256 + 512 * i, 512, T // 128) for i in range(8)]
        sbanks, obanks = [0, 1, 2, 3, 6, 7], [4, 5]
        si = oi = pi_ = 0
        for h in range(8):
            hb = h % 2
            p.dma("sp", KTh[hb][0:96, :], self.kT[h], w=[("KTh", hb)])
            p.dma("act", QTh[hb][0:96, :], self.qT[h], w=[("QTh", hb)])
            seq = [(qi, kt) for qi, (t0, n, nk) in enumerate(qtiles) for kt in range(nk)]
            sinfo = {}

            def emit_S(i):
                nonlocal si
                qi, kt = seq[i]
                t0, n, nk = qtiles[qi]
                b = sbanks[si % len(sbanks)]
                si += 1
                self.mm(self.PS[b][:, 0:n], KTh[hb][0:96, kt * 128:(kt + 1) * 128], QTh[hb][0:96, t0:t0 + n],
                        r=[("KTh", hb), ("QTh", hb)], w=[("ps", b)])
                sinfo[i] = b

            LOOK = 3
            for i in range(min(LOOK, len(seq))):
                emit_S(i)
            ob = None
            for i, (qi, kt) in enumerate(seq):
                t0, n, nk = qtiles[qi]
                if kt == 0:
                    ob = obanks[oi % 2]
                    oi += 1
                b = sinfo.pop(i)
                pt = pi_ % 4
                pi_ += 1
                self.act(PTt[pt][:, 0:n], self.PS[b][:, 0:n], AF.Exp, scale=scale, r=[("ps", b)], w=[("PTt", pt)])
                if i + LOOK < len(seq):
                    emit_S(i + LOOK)
                self.mm(self.PS[ob][:, 0:n], Vaug[:, kt, h, :], PTt[pt][:, 0:n], start=(kt == 0), stop=(kt == nk - 1),
                        r=rV + [("PTt", pt)], w=[("ps", ob)])
                if kt == nk - 1:
                    rb = oi % 2
                    p.op("dve", lambda e, o=Rr[rb][0:64, 0:n], i_=self.PS[ob][64:128, 0:n]: e.reciprocal(out=o, in_=i_),
                         r=[("ps", ob)], w=[("Rr", rb)])
                    self.tt("dve", AT[rb][0:64, 0:n], self.PS[ob][0:64, 0:n], Rr[rb][0:64, 0:n], ALU.mult,
                            r=[("ps", ob), ("Rr", rb)], w=[("AT", rb)])
                    p.dma("sp", self.mixT[h * 64:(h + 1) * 64, t0:t0 + n], AT[rb][0:64, 0:n], r=[("AT", rb)], w=[("mixT", h, qi)])
        p.barrier()

    def phase4(self, l, last):
        nc, p, ar, I = self.nc, self.p, self.arena, self.i
        ar.reset(self.m_phase)
        self.banks, self.pi = list(range(8)), 0
        self.bgrp = {0: [[0, 1, 2], 0], 1: [[3, 4, 5], 0], "s": [[6, 7], 0]}
        GM = ar.alloc([64, 10, 64], F32, "GM")
        p.dma("sp", GM[:], I["c_gmask"], w=[("GM",)])
        GBall = ar.alloc([64, NCH, 16], F32, "GBall")
        gbv = self.gb.rearrange("(n q) f -> q n f", q=64)
        for g_ in range(0, NCH, 8):
            ge = min(NCH, g_ + 8)
            p.dma("sp", GBall[:, g_:ge, :], gbv[:, g_:ge, :], w=[("GBall", g_)])
        p.op("dve", lambda e: e.nop(), r=[("GBall", g_) for g_ in range(0, NCH, 8)], w=[("GBall",)])
        OA = ar.alloc([128, 4, T], F32, "OA")
        for h in range(4):
            p.op("dve", lambda e, a=OA[:, h, :]: e.memset(a, 0.0), w=[("OA", n) for n in range(NCH)] if h == 0 else [])
        S = [ar.alloc([128, 4, 128], F32, "S") for _ in range(2)]
        for d in range(2):
            p.op("dve", lambda e, a=S[d][:]: e.memset(a, 0.0), w=[("S", d)])
        m_scan = ar.mark()
        QKt = [[ar.alloc([128, 12, 128], F32, "QKt") for _ in range(2)] for _ in range(2)]
        I64 = GM[:, 8, :]
        NEG4 = [ar.alloc([64, 4, 64], F32, "NEG4") for _ in range(2)]
        for d in range(2):
            self.cp("dve", NEG4[d][:], GM[:, 4 + d, :].unsqueeze(1).broadcast_to([64, 4, 64]), r=[("GM",)], w=[("NEG4",)])
        A64 = lambda sh: I64.unsqueeze(1).broadcast_to(sh)
        Bi, Bo = [], []
        for d in range(2):
            bi = dict(KV=ar.alloc([64, 8, 128], F32, "KV"), sm=ar.alloc([64, 8, 4], F32, "sm"),
                      GB2=ar.alloc([64, 4, 64], F32, "GB2"), DEC=ar.alloc([64, 4, 64], F32, "DEC"),
                      QKD=ar.alloc([64, 4, 64], F32, "QKD"), DS=ar.alloc([64, 4, 64], F32, "DS"),
                      M0=ar.alloc([64, 4, 64], F32, "M0"), N0=ar.alloc([64, 4, 64], F32, "N0"),
                      M=[ar.alloc([64, 4, 64], F32, "Mx") for _ in range(2)],
                      NP=[ar.alloc([64, 4, 128], F32, "NPx") for _ in range(2)],
                      DG2=ar.alloc([64, 4, 64], F32, "DG2"),
                      PRE=ar.alloc([64, 4, 128], F32, "PRE"), VN=ar.alloc([64, 4, 128], F32, "VN"))
            bo = [dict(PT5=ar.alloc([64, 4, 64], F32, "PT5"), qkT=ar.alloc([64, 4, 64], F32, "qkT"),
                       qdT=ar.alloc([128, 4, 64], F32, "qdT"), vb=ar.alloc([64, 4, 128], F32, "vb"),
                       ktl=ar.alloc([64, 4, 128], F32, "ktl"), kTc=ar.alloc([128, 4, 64], F32, "kTc"),
                       nbe=ar.alloc([64, 4], F32, "nbe"), gl=ar.alloc([128, 4], F32, "gl")) for _ in range(2)]
            Bi.append(bi)
            Bo.append(bo)
        P4S = int(os.environ.get("P4_STOP", "99"))
        order = [list(range(NCH)), list(range(NCC - 1, -1, -1)) + list(range(NCH - 1, NCC - 1, -1))]
        qv = self.qkvc.rearrange("(c q) t -> q c t", q=128)
        tile_buf = [dict(), dict()]
        nload = [0, 0]

        def ensure_tile(d, s):
            if s >= NCH:
                return
            tt_ = order[d][s] // 2
            if tt_ in tile_buf[d]:
                return
            b = nload[d] % 2
            nload[d] += 1
            for k_ in [k_ for k_, v_ in tile_buf[d].items() if v_ == b]:
                del tile_buf[d][k_]
            tile_buf[d][tt_] = b
            for hh in range(2):
                p.dma("sp" if d == 0 else "act", QKt[d][b][:, hh * 6:(hh + 1) * 6, :], qv[:, hh * 6:(hh + 1) * 6, tt_ * 128:(tt_ + 1) * 128],
                      w=[("QKt", d, b, hh)])

        def prepass(d, s):
            n = order[d][s]
            pb = s % 2
            bi, bo = Bi[d], Bo[d][pb]
            tb = tile_buf[d][n // 2]
            QK = QKt[d][tb]
            rQK = [("QKt", d, tb, 0), ("QKt", d, tb, 1)]
            c0 = (n % 2) * 64
            qT, kT, vT = QK[:, 0:4, c0:c0 + 64], QK[:, 4:8, c0:c0 + 64], QK[:, 8:12, c0:c0 + 64]
            A_d, B_d, NEGI, NSTR = GM[:, d, :], GM[:, 2 + d, :], GM[:, 4 + d, :], GM[:, 6 + d, :]
            g = GBall[:, n, d * 4:(d + 1) * 4]
            beta = GBall[:, n, 8 + d * 4:12 + d * 4]
            rG = [("GBall",), ("GM",)]
            ki = lambda nm: ("gi", d, nm)
            ko = lambda nm: ("go", d, pb, nm)
            sm = bi["sm"]
            gcs, e, dlt, ft = sm[:, 0, :], sm[:, 1, :], sm[:, 2, :], sm[:, 3, :]
            ps1, pk1 = self.bank(d)
            ps2, pk2 = self.bank(d)
            for h in range(4):
                self.tr(ps1[0:64, h * 128:(h + 1) * 128], kT[:, h, :], self.ident[:], r=rQK + [("ident",)], w=[pk1])
            for h in range(4):
                self.tr(ps2[0:64, h * 128:(h + 1) * 128], vT[:, h, :], self.ident[:], r=rQK + [("ident",)], w=[pk2])
            ps3, pk3 = self.bank(d)
            self.mm(ps3[0:64, 0:4], A_d, g, r=rG, w=[pk3])
            self.mm(ps3[:, 8:12], self.ones_f[0:64, :], g, r=rG + [("ones_f",)], w=[pk3])
            self.cp("dve", bi["KV"][:, 0:4, :], ps1[0:64, :].rearrange("q (h e) -> q h e", e=128), r=[pk1], w=[ki("K")])
            self.cp("act", bi["KV"][:, 4:8, :], ps2[0:64, :].rearrange("q (h e) -> q h e", e=128), r=[pk2], w=[ki("V")])
            yield
            if P4S <= 1:
                return
            self.cp("dve", gcs, ps3[0:64, 0:4], r=[pk3], w=[ki("gcs")])
            self.act(e, gcs, AF.Exp, r=[ki("gcs")], w=[ki("e")])
            self.tt("dve", dlt, ps3[0:64, 8:12], gcs, ALU.subtract, r=[pk3, ki("gcs")], w=[ki("dlt")])
            self.act(ft, dlt, AF.Exp, r=[ki("dlt")], w=[ki("ft")])
            self.cp("dve", bo["gl"][:], ps3[:, 8:12], r=[pk3], w=[ko("gl")])
            self.act(bo["gl"][:], bo["gl"][:], AF.Exp, r=[ko("gl")], w=[ko("gl")])
            self.stt(bo["nbe"][:], beta, -1.0, e, ALU.mult, ALU.mult, r=rG + [ki("e")], w=[ko("nbe")])
            self.tt("dve", bi["GB2"][:], g.unsqueeze(2).broadcast_to([64, 4, 64]), B_d.unsqueeze(1).broadcast_to([64, 4, 64]),
                    ALU.mult, r=rG, w=[ki("GB2")])
            ps4, pk4 = self.bank(d)
            self.mm(ps4[0:64, 0:256], A_d, bi["GB2"][:].rearrange("q h e -> q (h e)"), start=True, stop=False, r=rG + [ki("GB2")], w=[pk4])
            self.mm(ps4[0:64, 0:256], I64, NEG4[d][:].rearrange("q h e -> q (h e)"),
                    start=False, stop=True, r=rG + [("NEG4",)], w=[pk4])
            self.act(bi["DEC"][:].rearrange("q h e -> q (h e)"), ps4[0:64, 0:256], AF.Exp, r=[pk4], w=[ki("DEC")])
            ps5, pk5 = self.bank(d)
            ps6, pk6 = self.bank(d)
            for h in range(4):
                self.mm(ps5[0:64, h * 64:(h + 1) * 64], kT[:, h, :], kT[:, h, :], r=rQK, w=[pk5])
            for h in range(4):
                self.mm(ps6[0:64, h * 64:(h + 1) * 64], qT[:, h, :], kT[:, h, :], r=rQK, w=[pk6])
            yield
            if P4S <= 2:
                return
            self.tt("dve", bi["DG2"][:], A64([64, 4, 64]), e.unsqueeze(2).broadcast_to([64, 4, 64]), ALU.mult, r=rG + [ki("e")], w=[ki("DG2")])
            psE, pkE = self.bank(d)
            self.mm(psE[:, 0:256], self.ones_f[0:64, :], bi["DG2"][:].rearrange("q h e -> q (h e)"), r=[ki("DG2"), ("ones_f",)], w=[pkE])
            self.tt("dve", bo["qdT"][:], qT, psE[:, 0:256].rearrange("q (h e) -> q h e", e=64), ALU.mult, r=rQK + [pkE], w=[ko("qdT")])
            for h in range(4):
                self.act(bo["vb"][:, h, :], bi["KV"][:, 4 + h, :], AF.Copy, scale=beta[:, h:h + 1], r=rG + [ki("V")], w=[ko("vb")])
                self.act(bo["ktl"][:, h, :], bi["KV"][:, h, :], AF.Copy, scale=ft[:, h:h + 1], r=[ki("ft"), ki("K")], w=[ko("ktl")])
            self.cp("act", bo["kTc"][:], kT, r=rQK, w=[ko("kTc")])
            yield
            if P4S <= 3:
                return
            v4 = lambda ps_: ps_[0:64, 0:256].rearrange("q (h e) -> q h e", e=64)
            self.tt("dve", bi["QKD"][:], v4(ps6), bi["DEC"][:], ALU.mult, r=[pk6, ki("DEC")], w=[ki("QKD")])
            self.tt("dve", bi["DS"][:], bi["DEC"][:], NSTR.unsqueeze(1).broadcast_to([64, 4, 64]), ALU.mult, r=rG + [ki("DEC")], w=[ki("DS")])
            self.tt("dve", bi["DS"][:], v4(ps5), bi["DS"][:], ALU.mult, r=[pk5, ki("DS")], w=[ki("DS")])
            self.tt("dve", bi["M0"][:], bi["DS"][:], beta.unsqueeze(2).broadcast_to([64, 4, 64]), ALU.mult, r=rG + [ki("DS")], w=[ki("M0")])
            ps7, pk7 = self.bank(d)
            ps8, pk8 = self.bank(d)
            for h in range(4):
                self.tr(ps7[0:64, h * 64:(h + 1) * 64], bi["M0"][:, h, :], self.ident[0:64, 0:64], r=[ki("M0"), ("ident",)], w=[pk7])
            for h in range(4):
                self.tr(ps8[0:64, h * 64:(h + 1) * 64], bi["QKD"][:, h, :], self.ident[0:64, 0:64], r=[ki("QKD"), ("ident",)], w=[pk8])
            self.cp("act", bi["N0"][:], v4(ps7), r=[pk7], w=[ki("N0")])
            self.cp("dve", bo["qkT"][:], v4(ps8), r=[pk8], w=[ko("qkT")])
            yield
            if P4S <= 4:
                return
            M, NP = bi["M"], bi["NP"]
            psA, pkA = self.bank(d)
            psB, pkB = self.bank(d)
            for h in range(4):
                self.mm(psA[0:64, h * 64:(h + 1) * 64], bi["N0"][:, h, :], bi["M0"][:, h, :], r=[ki("N0"), ki("M0")], w=[pkA])
            for h in range(4):
                self.mm(psB[0:64, h * 64:(h + 1) * 64], bi["M0"][:, h, :], bi["N0"][:, h, :], r=[ki("N0"), ki("M0")], w=[pkB])
            self.cp("act", M[0][:], v4(psA), r=[pkA], w=[ki("M_0")])
            self.cp("dve", NP[0][:, :, 0:64], v4(psB), r=[pkB], w=[ki("NPn_0")])
            self.tt("dve", NP[0][:, :, 64:128], bi["N0"][:], A64([64, 4, 64]), ALU.add, r=rG + [ki("N0")], w=[ki("NPp_0")])
            yield
            for lev in range(1, 1 + int(os.environ.get('P4_LEV', '5'))):
                x, y = (lev - 1) % 2, lev % 2
                psY, pkY = self.bank(d)
                rx = [ki("M_%d" % x), ki("NPn_%d" % x), ki("NPp_%d" % x)]
                if lev < 5:
                    for h in range(4):
                        self.mm(psY[0:64, h * 128:(h + 1) * 128], M[x][:, h, :], NP[x][:, h, :], r=rx, w=[pkY])
                    psM, pkM = self.bank(d)
                    for h in range(4):
                        self.mm(psM[0:64, h * 64:(h + 1) * 64], NP[x][:, h, 0:64], M[x][:, h, :], r=rx, w=[pkM])
                else:
                    for h in range(4):
                        self.mm(psY[0:64, h * 128 + 64:(h + 1) * 128], M[x][:, h, :], NP[x][:, h, 64:128], r=rx, w=[pkY])
                Y3 = psY[0:64, :].rearrange("q (h e) -> q h e", e=128)
                if lev < 5:
                    self.tt("dve", NP[y][:, :, 64:128], Y3[:, :, 64:128], NP[x][:, :, 64:128], ALU.add, r=[pkY, ki("NPp_%d" % x)], w=[ki("NPp_%d" % y)])
                    self.cp("dve", NP[y][:, :, 0:64], Y3[:, :, 0:64], r=[pkY], w=[ki("NPn_%d" % y)])
                    self.cp("act", M[y][:], v4(psM), r=[pkM], w=[ki("M_%d" % y)])
                else:
                    self.tt("dve", bo["PT5"][:], Y3[:, :, 64:128], NP[x][:, :, 64:128], ALU.add, r=[pkY, ki("NPp_%d" % x)], w=[ko("PT5")])
                yield

        def scan(s):
            pb = s % 2
            ns = [order[d][s] for d in range(2)]
            ko = lambda d, nm: ("go", d, pb, nm)
            ki = lambda d, nm: ("gi", d, nm)
            bank = {}
            for d in range(2):
                bo = Bo[d][pb]
                bank[d] = self.bank("s")
                for h in range(4):
                    self.mm(bank[d][0][0:64, h * 128:(h + 1) * 128], bo["kTc"][:, h, :], S[d][:, h, :], r=[ko(d, "kTc"), ("S", d)], w=[bank[d][1]])
            yield
            for d in range(2):
                bo, bi = Bo[d][pb], Bi[d]
                ps_, pk_ = bank[d]
                self.tt("dve", bi["PRE"][:], ps_[0:64, :].rearrange("q (h e) -> q h e", e=128), bo["nbe"][:].unsqueeze(2).broadcast_to([64, 4, 128]),
                        ALU.mult, r=[pk_, ko(d, "nbe")], w=[ki(d, "PRE")])
                self.tt("dve", bi["PRE"][:], bi["PRE"][:], bo["vb"][:], ALU.add, r=[ki(d, "PRE"), ko(d, "vb")], w=[ki(d, "PRE")])
            yield
            for d in range(2):
                bo, bi = Bo[d][pb], Bi[d]
                bank[d] = self.bank("s")
                for h in range(4):
                    self.mm(bank[d][0][0:64, h * 128:(h + 1) * 128], bo["PT5"][:, h, :], bi["PRE"][:, h, :], r=[ko(d, "PT5"), ki(d, "PRE")], w=[bank[d][1]])
            yield
            for d in range(2):
                bi = Bi[d]
                self.cp("act", bi["VN"][:], bank[d][0][0:64, :].rearrange("q (h e) -> q h e", e=128), r=[bank[d][1]], w=[ki(d, "VN")])
            yield
            for d in range(2):
                bo, bi = Bo[d][pb], Bi[d]
                po, pko = self.bank("s")
                for h in range(4):
                    self.mm(po[:, h * 64:(h + 1) * 64], S[d][:, h, :], bo["qdT"][:, h, :], start=True, stop=False, r=[("S", d), ko(d, "qdT")], w=[pko])
                    self.mm(po[:, h * 64:(h + 1) * 64], bi["VN"][:, h, :], bo["qkT"][:, h, :], start=False, stop=True, r=[ki(d, "VN"), ko(d, "qkT")], w=[pko])
                pss, pks = self.bank("s")
                for h in range(4):
                    self.mm(pss[:, h * 128:(h + 1) * 128], bo["ktl"][:, h, :], bi["VN"][:, h, :], r=[ko(d, "ktl"), ki(d, "VN")], w=[pks])
                yield
                n = ns[d]
                oa = OA[:, :, n * 64:(n + 1) * 64]
                self.tt("dve", oa, po[:, 0:256].rearrange("q (h e) -> q h e", e=64), oa, ALU.add, r=[pko, ("OA", n)], w=[("OA", n)])
                for h in range(4):
                    self.stt(S[d][:, h, :], S[d][:, h, :], bo["gl"][:, h:h + 1], pss[:, h * 128:(h + 1) * 128], ALU.mult, ALU.add,
                             r=[("S", d), ko(d, "gl"), pks], w=[("S", d)])
            yield

        def run_gens(gens):
            gens = list(gens)
            while gens:
                for g_ in list(gens):
                    try:
                        next(g_)
                    except StopIteration:
                        gens.remove(g_)

        nsteps = NCH if self.upto != "p4s" else 6
        if P4S <= 0:
            p.barrier()
            return
        for d in range(2):
            ensure_tile(d, 0)
            ensure_tile(d, 1)
        run_gens([prepass(0, 0), prepass(1, 0)])
        if self.debug and l == 0 and P4S >= 99:
            for d in range(2):
                for nm, t_ in list(Bo[d][0].items()) + [(k_, Bi[d][k_]) for k_ in ("M0", "N0", "DEC", "KV", "QKD", "sm")]:
                    src_ = t_[:, 0:4, :] if nm == "sm" else t_[:]
                    dt_ = nc.dram_tensor("dbg_%d_%s" % (d, nm), list(src_.shape), F32, kind="ExternalOutput").ap()
                    p.dma("sp", dt_, src_, r=[("go", d, 0, nm), ("gi", d, nm), ("gi", d, "K"), ("gi", d, "V"), ("gi", d, "e"), ("gi", d, "ft")])
        for s in range(nsteps):
            gens = [scan(s)] if P4S >= 6 else []
            if s + 1 < nsteps:
                for d in range(2):
                    ensure_tile(d, s + 2)
                gens = [prepass(0, s + 1), prepass(1, s + 1)] + gens
            run_gens(gens)
        if self.debug and l == 0 and P4S >= 99:
            dt_ = nc.dram_tensor("dbg_OA", [128, 4, T], F32, kind="ExternalOutput").ap()
            p.dma("sp", dt_, OA[:], r=[("OA", n) for n in range(NCH)])
        p.barrier()
        ar.reset(m_scan)
        ZSt = [ar.alloc([128, 4, 512], BF16, "ZSt") for _ in range(2)]
        SQ = [ar.alloc([128, 512], BF16, "SQ") for _ in range(2)]
        LN = ar.alloc([128, 512], F32, "LN")
        RS = [ar.alloc([128, 512], F32, "RS") for _ in range(2)]
        Y1 = [ar.alloc([128, 512], F32, "Y1") for _ in range(2)]
        Yo = [ar.alloc([128, 4, 512], BF16, "Yo") for _ in range(2)]
        tiles = ([] if last else [(0, 256)]) + [(256 + 512 * i, 512) for i in range(8)]
        zv = self.zs.rearrange("(c q) t -> q c t", q=128)
        mv = self.mixT[512:1024, :].rearrange("(c q) t -> q c t", q=128)
        for ti, (t0, n) in enumerate(tiles):
            b = ti % 2
            p.dma("act", ZSt[b][:, :, 0:n], zv[:, :, t0:t0 + n], w=[("ZSt4", b)])
            rOA = [("OA", c) for c in range(t0 // 64, (t0 + n) // 64)]
            for h in range(4):
                hb = h % 2
                self.tt("dve", SQ[hb][:, 0:n], OA[:, h, t0:t0 + n], OA[:, h, t0:t0 + n], ALU.mult, r=rOA, w=[("SQ4", hb)])
                ps, pk = self.bank()
                self.mm(ps[:, 0:n], self.ones_bf[:], SQ[hb][:, 0:n], r=[("SQ4", hb), ("ones_bf",)], w=[pk])
                self.rsqrt_from_psum(RS[hb][:, 0:n], ps[:, 0:n], 128.0, n, r=[pk], w=[("RS4", hb)], tmp=LN[:, 0:n])
                self.stt(Y1[hb][:, 0:n], OA[:, h, t0:t0 + n], self.gngT[:, l:l + 1], RS[hb][:, 0:n], ALU.mult, ALU.mult,
                         r=rOA + [("RS4", hb), ("gngT",)], w=[("Y1", hb)])
                self.tt("dve", Yo[b][:, h, 0:n], Y1[hb][:, 0:n], ZSt[b][:, h, 0:n], ALU.mult, r=[("Y1", hb), ("ZSt4", b)], w=[("Yo", b, h)])
            p.dma("sp", mv[:, :, t0:t0 + n], Yo[b][:, :, 0:n], r=[("Yo", b, h) for h in range(4)], w=[("mixT4", ti)])
        p.barrier()

    def phase56(self, l, last):
        nc, p, ar, I = self.nc, self.p, self.arena, self.i
        ar.reset(self.m_phase)
        self.banks, self.pi = list(range(8)), 0
        Wo = ar.alloc([128, 8, DM], BF16, "Wo")
        W1 = ar.alloc([128, 8, 4 * DM], BF16, "W1")
        W2 = ar.alloc([128, 32, DM], BF16, "W2")
        mW = ar.mark()
        stg = [ar.alloc([128, 2048], F32, "stg") for _ in range(3)]
        self.wkeys, self.stg_i = {}, 0
        self.load_weight_bf16(Wo[:], I["w_out"][l].rearrange("(k q) f -> q k f", q=128), ("Wo",), stg)
        self.load_weight_bf16(W1[:], I["w_ff1"][l].rearrange("(k q) f -> q k f", q=128), ("W1",), stg)
        self.load_weight_bf16(W2[:], I["w_ff2"][l].rearrange("(k q) f -> q k f", q=128), ("W2",), stg)
        p.barrier()
        ar.reset(mW)
        NT = 256
        X = [ar.alloc([128, 8, NT], F32, "X") for _ in range(2)]
        MT = [ar.alloc([128, 8, NT], BF16, "MT") for _ in range(2)]
        HT2 = ar.alloc([128, 8, NT], BF16, "HT2")
        AT = ar.alloc([128, 32, NT], BF16, "AT")
        SQ = AT[:, 0:8, :]
        kSQ = [("AT", k) for k in range(8)]
        T1 = [ar.alloc([128, NT], F32, "T1") for _ in range(2)]
        RL = [ar.alloc([128, NT], F32, "RL") for _ in range(2)]
        LN = ar.alloc([128, NT], F32, "LN")
        RS = ar.alloc([128, NT], F32, "RS")
        xTv = self.xT.rearrange("(c q) t -> q c t", q=128)
        mTv = self.mixT.rearrange("(c q) t -> q c t", q=128)
        tiles = [(NT * i, NT, 0 if i == 0 else 1) for i in range(T // NT)]
        if last:
            tiles = tiles[1:]
        n = NT
        for ti, (t0, n_, col) in enumerate(tiles):
            b = ti % 2
            Xt = X[b]
            kX = [("X", b, k) for k in range(8)]
            p.dma("sp", Xt[:], xTv[:, :, t0:t0 + n], w=kX)
            p.dma("act", MT[b][:], mTv[:, :, t0:t0 + n], w=[("MT", b)])
            for dc in range(8):
                ps, pk = self.bank()
                for k in range(8):
                    self.mm(ps[:, 0:n], Wo[:, k, dc * 128:(dc + 1) * 128], MT[b][:, k, :], start=(k == 0), stop=(k == 7),
                            r=[("Wo",), ("MT", b)], w=[pk])
                self.stt(Xt[:, dc, :], ps[:, 0:n], self.modv(l, 2, dc, col), Xt[:, dc, :], ALU.mult, ALU.add,
                         r=[pk, ("X", b, dc), ("modT", l)], w=[("X", b, dc)])
            self.tt("dve", SQ, Xt[:], Xt[:], ALU.mult, r=kX, w=kSQ)
            ps, pk = self.bank()
            for k in range(8):
                self.mm(ps[:, 0:n], self.ones_bf[:], SQ[:, k, :], start=(k == 0), stop=(k == 7), r=kSQ + [("ones_bf",)], w=[pk])
            self.rsqrt_from_psum(RS[:], ps[:, 0:n], float(DM), n, r=[pk], w=[("RS5",)], tmp=LN[:])
            for k in range(8):
                tb = k % 2
                self.stt(T1[tb][:], Xt[:, k, :], self.A2[:, l, k, col:col + 1], RS[:], ALU.mult, ALU.mult,
                         r=[("X", b, k), ("RS5",), ("A", l, 32)], w=[("T15", tb)])
                self.act(HT2[:, k, :], T1[tb][:], AF.Identity, bias=self.modv(l, 3, k, col), r=[("T15", tb), ("modT", l)], w=[("HT2", k)])
            rH = [("HT2", k) for k in range(8)]
            for fc in range(32):
                ps, pk = self.bank()
                for k in range(8):
                    self.mm(ps[:, 0:n], W1[:, k, fc * 128:(fc + 1) * 128], HT2[:, k, :], start=(k == 0), stop=(k == 7), r=rH + [("W1",)], w=[pk])
                rb = fc % 2
                self.act(RL[rb][:], ps[:, 0:n], AF.Relu, r=[pk], w=[("RL", rb)])
                self.tt("dve", AT[:, fc, :], RL[rb][:], RL[rb][:], ALU.mult, r=[("RL", rb)], w=[("AT", fc)])
            rA = [("AT", fc) for fc in range(32)]
            for dc in range(8):
                ps, pk = self.bank()
                for fc in range(32):
                    self.mm(ps[:, 0:n], W2[:, fc, dc * 128:(dc + 1) * 128], AT[:, fc, :], start=(fc == 0), stop=(fc == 31), r=rA + [("W2",)], w=[pk])
                self.stt(Xt[:, dc, :], ps[:, 0:n], self.modv(l, 5, dc, col), Xt[:, dc, :], ALU.mult, ALU.add,
                         r=[pk, ("X", b, dc), ("modT", l)], w=[("X", b, dc)])
            p.dma("sp", xTv[:, :, t0:t0 + n], Xt[:], r=kX, w=[("xTo", ti)])
        p.barrier()

    def epilogue(self):
        nc, p, ar, I = self.nc, self.p, self.arena, self.i
        ar.reset(self.m_phase)
        self.banks, self.pi = list(range(8)), 0
        X = [ar.alloc([128, 8, 512], F32, "X") for _ in range(2)]
        SQ = ar.alloc([128, 8, 512], BF16, "SQ")
        Y = ar.alloc([128, 8, 512], F32, "Y")
        LN = ar.alloc([128, 512], F32, "LN")
        RS = ar.alloc([128, 512], F32, "RS")
        OUT = [ar.alloc([128, DM], F32, "OUT") for _ in range(2)]
        xTv = self.xT.rearrange("(c q) t -> q c t", q=128)
        n = 512
        for ti in range(T_L // 512):
            t0 = T_C + ti * 512
            b = ti % 2
            p.dma("sp", X[b][:], xTv[:, :, t0:t0 + n], w=[("X", b)])
            self.tt("dve", SQ[:], X[b][:], X[b][:], ALU.mult, r=[("X", b)], w=[("SQ",)])
            ps, pk = self.bank()
            for k in range(8):
                self.mm(ps[:, 0:n], self.ones_bf[:], SQ[:, k, :], start=(k == 0), stop=(k == 7), r=[("SQ",), ("ones_bf",)], w=[pk])
            self.rsqrt_from_psum(RS[:], ps[:, 0:n], float(DM), n, r=[pk], w=[("RSe",)], tmp=LN[:])
            for k in range(8):
                self.stt(Y[:, k, :], X[b][:, k, :], self.fngT[:, k:k + 1], RS[:], ALU.mult, ALU.mult, r=[("X", b), ("RSe",), ("fngT",)], w=[("Y", k)])
            for j in range(4):
                ob = (ti * 4 + j) % 2
                for hb in range(2):
                    ps, pk = self.bank()
                    for c in range(4):
                        k = hb * 4 + c
                        self.tr(ps[:, c * 128:(c + 1) * 128], Y[:, k, j * 128:(j + 1) * 128], self.ident[:], r=[("Y", k), ("ident",)], w=[pk])
                    self.cp("act" if hb == 0 else "dve", OUT[ob][:, hb * 512:(hb + 1) * 512], ps[:, :], r=[pk], w=[("OUT", ob, hb)])
                r0 = ti * 512 + j * 128
                p.dma("sp", self.out[r0:r0 + 128, :], OUT[ob][:], r=[("OUT", ob, 0), ("OUT", ob, 1)], w=[("out", ti, j)])


def _consts():
    ident = np.eye(128, dtype=np.float32)
    t = np.arange(T_L)
    row = (t // 64).astype(np.float32)
    col = (t % 64).astype(np.float32)
    inv_freq = (np.float32(10000.0) ** (-np.arange(8, dtype=np.float32) / np.float32(8))).astype(np.float32)
    ang = np.concatenate([row[:, None] * inv_freq, col[:, None] * inv_freq], axis=-1).astype(np.float32)
    cos = np.cos(ang).astype(np.float32).T
    sin = np.sin(ang).astype(np.float32).T
    ropec = np.ones((32, T), np.float32)
    ropes = np.zeros((32, T), np.float32)
    ropec[0:16, T_C:] = cos
    ropec[16:32, T_C:] = cos
    ropes[0:16, T_C:] = sin
    ropes[16:32, T_C:] = sin
    ii = np.arange(64)
    gm = np.zeros((64, 10, 64), np.float32)
    A0 = ii[:, None] <= ii[None, :]
    A1 = ii[:, None] >= ii[None, :]
    B0 = ii[:, None] > ii[None, :]
    B1 = ii[:, None] < ii[None, :]
    gm[:, 0], gm[:, 1], gm[:, 2], gm[:, 3] = A0, A1, B0, B1
    gm[:, 4] = np.where(A0.T, 0.0, -30000.0)
    gm[:, 5] = np.where(A1.T, 0.0, -30000.0)
    gm[:, 6] = -B0.astype(np.float32)
    gm[:, 7] = -B1.astype(np.float32)
    gm[:, 8] = np.eye(64, dtype=np.float32)
    return dict(c_ident=ident, c_ropec=ropec, c_ropes=ropes, c_gmask=gm)


def _in_map(inputs, b, consts):
    f = lambda a: np.ascontiguousarray(np.asarray(a, dtype=np.float32))
    m = dict(
        x=f(inputs["x"][b]), ctx=f(inputs["ctx"][b]),
        cc=f(np.stack([np.asarray(inputs["c_ctx"]), np.asarray(inputs["c"][b])], 0)),
        a_log=f(np.asarray(inputs["a_log"]).reshape(L, 8)), dt_bias=f(np.asarray(inputs["dt_bias"]).reshape(L, 8)),
        final_norm_g=f(np.asarray(inputs["final_norm_g"]).reshape(1, DM)),
    )
    for k in ("w_ada", "b_ada", "norm1_g", "w_in", "q_a_g", "w_q_b", "kv_a_g", "w_kv_b", "conv_w", "gdn_norm_g",
              "w_out", "norm2_g", "w_ff1", "w_ff2"):
        m[k] = f(inputs[k])
    m.update(consts)
    return m


def kernel(**inputs):
    nc = Builder().build()
    consts = _consts()
    in_maps = [_in_map(inputs, b, consts) for b in range(8)]
    res = run_bass_kernel_spmd(nc, in_maps, core_ids=list(range(8)))
    return np.stack([np.asarray(r["out"], dtype=np.float32) for r in res.results], 0)
```

```python
import contextlib
import os
import numpy as np
import concourse.bass as bass
import concourse.mybir as mybir
from concourse.bass_utils import run_bass_kernel_spmd

F32 = mybir.dt.float32
BF16 = mybir.dt.bfloat16
AF = mybir.ActivationFunctionType
ALU = mybir.AluOpType

SAME_ENGINE_SYNC = True
N_DMA_SEMS = 24
N_HW_SEMS = 16

T_C, T_L = 256, 4096
T = T_C + T_L
DM = 1024
L = 4
EPS = 1e-6
D_IN = 2480
NCH = T // 64
NCC = T_C // 64


class _Op:
    __slots__ = ("eng", "fn", "deps", "is_dma", "sem_j", "tok", "signal", "idx")


class Prog:
    ENGS = ("pe", "act", "dve", "pool", "sp")

    def __init__(self, nc):
        self.nc = nc
        self.ops = []
        self.state = {}
        self.dma_last = [None] * N_DMA_SEMS
        self.dma_rr = 0
        self.sw_rr = 0
        self.pending_barrier = {}
        self.last_op = {}

    def _sem_key(self, op):
        return ("d", op.sem_j) if op.is_dma else ("e", op.eng)

    def _track(self, op, reads, writes):
        deps = []
        for k in reads:
            st = self.state.get(k)
            if st is not None and st[0] is not None:
                deps.append(st[0])
        for k in writes:
            st = self.state.get(k)
            if st is not None:
                if st[0] is not None:
                    deps.append(st[0])
                deps.extend(st[1].values())
        for k in reads:
            st = self.state.get(k)
            if st is None:
                st = self.state[k] = [None, {}]
            st[1][self._sem_key(op)] = op
        for k in writes:
            self.state[k] = [op, {}]
        pb = self.pending_barrier.pop(op.eng, None)
        if pb:
            deps.extend(pb)
        op.deps = [d for d in deps if d is not op]

    def op(self, eng, fn, r=(), w=()):
        o = _Op()
        o.eng, o.fn, o.is_dma, o.sem_j, o.signal, o.tok = eng, fn, False, None, False, None
        o.idx = len(self.ops)
        self._track(o, r, w)
        self.ops.append(o)
        self.last_op[("e", eng)] = o
        return o

    def dma(self, eng, out, in_, r=(), w=(), **kw):
        o = _Op()
        o.eng, o.is_dma, o.signal, o.tok = eng, True, True, None
        if eng == "pool":
            j = N_HW_SEMS + self.sw_rr
            self.sw_rr = (self.sw_rr + 1) % (N_DMA_SEMS - N_HW_SEMS)
        else:
            j = self.dma_rr
            self.dma_rr = (j + 1) % N_HW_SEMS
        o.sem_j = j
        o.fn = lambda e, out=out, in_=in_, kw=kw: e.dma_start(out=out, in_=in_, **kw)
        o.idx = len(self.ops)
        self._track(o, r, w)
        if self.dma_last[j] is not None:
            o.deps.append(self.dma_last[j])
        self.dma_last[j] = o
        self.ops.append(o)
        self.last_op[("d", j)] = o
        return o

    def barrier(self):
        frontier = list(self.last_op.values())
        for e in self.ENGS:
            self.pending_barrier[e] = list(frontier)

    def finish(self, eng="sp"):
        deps = [o for o in self.dma_last if o is not None]
        o = self.op(eng, lambda e: e.nop(), r=(), w=())
        o.deps.extend(deps)
        return o

    def emit(self, stack):
        nc = self.nc
        for o in self.ops:
            keep = []
            for d in o.deps:
                if not d.is_dma and d.eng == o.eng and not o.is_dma:
                    if d.eng == "pe" or not SAME_ENGINE_SYNC:
                        continue
                d.signal = True
                keep.append(d)
            o.deps = keep
        esem = {e: stack.enter_context(nc.semaphore("s_" + e)) for e in self.ENGS}
        dsem = [stack.enter_context(nc.semaphore("s_dma%d" % j)) for j in range(N_DMA_SEMS)]
        cnt = {e: 0 for e in self.ENGS}
        dcnt = [0] * N_DMA_SEMS
        for o in self.ops:
            if o.is_dma:
                dcnt[o.sem_j] += 16
                o.tok = (dsem[o.sem_j], dcnt[o.sem_j], ("d", o.sem_j))
            elif o.signal:
                cnt[o.eng] += 1
                o.tok = (esem[o.eng], cnt[o.eng], ("e", o.eng))
        by_eng = {e: [o for o in self.ops if o.eng == e] for e in self.ENGS}

        def body(ename):
            def f(eng):
                waited = {}
                for o in by_eng[ename]:
                    need = {}
                    for d in o.deps:
                        sem, val, sk = d.tok
                        if waited.get(sk, 0) < val and need.get(sk, (None, 0))[1] < val:
                            need[sk] = (sem, val)
                    for sk, (sem, val) in need.items():
                        eng.wait_ge(sem, val)
                        waited[sk] = val
                    ins = o.fn(eng)
                    if o.is_dma:
                        ins.then_inc(o.tok[0], 16)
                    elif o.signal:
                        ins.then_inc(o.tok[0], 1)
            return f

        with nc.Block() as block:
            block.tensor(body("pe"))
            block.scalar(body("act"))
            block.vector(body("dve"))
            block.gpsimd(body("pool"))
            block.sync(body("sp"))


class Arena:
    def __init__(self, nc, lo, hi):
        self.nc, self.lo, self.hi, self.cur, self.n = nc, lo, hi, lo, 0

    def alloc(self, shape, dtype, name="t"):
        nbytes = int(np.prod(shape[1:])) * mybir.dt.size(dtype)
        off = (self.cur + 63) // 64 * 64
        assert off + nbytes <= self.hi, ("SBUF overflow", name, off + nbytes, self.hi)
        self.cur = off + nbytes
        self.n += 1
        return self.nc.alloc_sbuf_tensor_at("%s_%d" % (name, self.n), list(shape), dtype, offset=off)

    def mark(self):
        return self.cur

    def reset(self, m):
        self.cur = m


class Builder:
    def __init__(self, debug=False, upto=None, nlayers=L):
        self.debug, self.upto, self.nlayers = debug, upto, nlayers
        nc = self.nc = bass.Bass("TRN2", target_bir_lowering=False)
        self.p = Prog(nc)
        self.uid = 0
        din = lambda n, sh, dt=F32: nc.dram_tensor(n, sh, dt, kind="ExternalInput").ap()
        skind = "ExternalOutput" if debug else "Internal"
        dsc = lambda n, sh, dt: nc.dram_tensor(n, sh, dt, kind=skind).ap()
        self.i = dict(
            x=din("x", [T_L, DM]), ctx=din("ctx", [T_C, DM]), cc=din("cc", [2, DM]),
            w_ada=din("w_ada", [L, DM, 6 * DM]), b_ada=din("b_ada", [L, 6 * DM]),
            norm1_g=din("norm1_g", [L, DM]), w_in=din("w_in", [L, DM, D_IN]),
            q_a_g=din("q_a_g", [L, 256]), w_q_b=din("w_q_b", [L, 256, 768]),
            kv_a_g=din("kv_a_g", [L, 128]), w_kv_b=din("w_kv_b", [L, 128, 1024]),
            conv_w=din("conv_w", [L, 5, 1536]), a_log=din("a_log", [L, 8]), dt_bias=din("dt_bias", [L, 8]),
            gdn_norm_g=din("gdn_norm_g", [L, 128]), w_out=din("w_out", [L, DM, DM]),
            norm2_g=din("norm2_g", [L, DM]), w_ff1=din("w_ff1", [L, DM, 4 * DM]),
            w_ff2=din("w_ff2", [L, 4 * DM, DM]), final_norm_g=din("final_norm_g", [1, DM]),
            c_ident=din("c_ident", [128, 128]), c_ropec=din("c_ropec", [32, T]), c_ropes=din("c_ropes", [32, T]),
            c_gmask=din("c_gmask", [64, 10, 64]),
        )
        self.out = nc.dram_tensor("out", [T_L, DM], F32, kind="ExternalOutput").ap()
        self.xT = dsc("xT", [DM, T], F32)
        self.pqkv = dsc("pqkv", [1536, T], BF16)
        self.zs = dsc("zs", [512, T], BF16)
        self.gb = dsc("gb", [T, 16], F32)
        self.qT = dsc("qT", [8, 96, T], BF16)
        self.kT = dsc("kT", [8, 96, T], BF16)
        self.vtok = dsc("vtok", [T, 8, 64], BF16)
        self.qkvc = dsc("qkvc", [1536, T], F32)
        self.mixT = dsc("mixT", [DM, T], BF16)
        if debug:
            self.dbg_mod = nc.dram_tensor("dbg_mod", [128, L * 96], F32, kind="ExternalOutput").ap()

    def key(self, *a):
        return a

    def bank(self, grp=None):
        if grp is not None:
            lst, i = self.bgrp[grp]
            self.bgrp[grp][1] = (i + 1) % len(lst)
            b = lst[i]
            return self.PS[b], ("ps", b)
        i = self.pi
        self.pi = (i + 1) % len(self.banks)
        b = self.banks[i]
        return self.PS[b], ("ps", b)

    def mm(self, out, lhsT, rhs, start=True, stop=True, r=(), w=()):
        return self.p.op("pe", lambda e: e.matmul(out, lhsT, rhs, start=start, stop=stop), r, w)

    def tr(self, out, in_, ident, r=(), w=()):
        return self.p.op("pe", lambda e: e.transpose(out, in_, ident), r, w)

    def act(self, out, in_, func, scale=1.0, bias=0.0, r=(), w=()):
        return self.p.op("act", lambda e: e.activation(out=out, in_=in_, func=func, bias=bias, scale=scale), r, w)

    def tt(self, eng, out, in0, in1, op, r=(), w=()):
        return self.p.op(eng, lambda e: e.tensor_tensor(out=out, in0=in0, in1=in1, op=op), r, w)

    def ts(self, eng, out, in0, s1, op0, s2=None, op1=None, r=(), w=()):
        if op1 is None:
            return self.p.op(eng, lambda e: e.tensor_scalar(out=out, in0=in0, scalar1=s1, scalar2=None, op0=op0), r, w)
        return self.p.op(eng, lambda e: e.tensor_scalar(out=out, in0=in0, scalar1=s1, scalar2=s2, op0=op0, op1=op1), r, w)

    def stt(self, out, in0, scalar, in1, op0, op1, r=(), w=()):
        return self.p.op("dve", lambda e: e.scalar_tensor_tensor(out=out, in0=in0, scalar=scalar, in1=in1, op0=op0, op1=op1), r, w)

    def cp(self, eng, out, in_, r=(), w=()):
        if eng == "act":
            return self.p.op("act", lambda e: e.copy(out=out, in_=in_), r, w)
        return self.p.op(eng, lambda e: e.tensor_copy(out=out, in_=in_), r, w)

    def rsqrt_from_psum(self, dst, ps, n_div, n, r, w, tmp):
        self.act(tmp, ps, AF.Ln, scale=1.0 / n_div, bias=self.eps_ap, r=list(r) + [("eps",)], w=[("tmp", tmp.tensor.name)])
        self.act(dst, tmp, AF.Exp, scale=-0.5, r=[("tmp", tmp.tensor.name)], w=w)

    def load_rows_T(self, rows_ap, nrows, dst, dkey):
        p = self.p
        self.uid += 1
        k = ("lrt", self.uid)
        tmp = self.lrt_tmp
        p.dma("sp", tmp[0:nrows, :], rows_ap, w=[("lrt_tmp",)])
        ps, pk = self.bank()
        self.tr(ps[:, 0:nrows], tmp[0:nrows, :], self.ident[0:nrows, 0:nrows], r=[("lrt_tmp",), ("ident",)], w=[pk])
        self.cp("dve", dst, ps[:, 0:nrows], r=[pk], w=[dkey])

    def build(self):
        nc, p = self.nc, self.p
        st = contextlib.ExitStack()
        with st:
            self.PS = [st.enter_context(nc.psum_tensor("ps%d" % i, [128, 512], F32)) for i in range(8)]
            self.banks, self.pi = list(range(8)), 0
            self.arena = Arena(nc, (nc.sbuf_base + 63) // 64 * 64, nc.sbuf_top)
            self.prologue()
            for l in range(self.nlayers):
                last = l == L - 1
                if self.upto == "prologue":
                    break
                self.phase1(l)
                if self.upto == "p1":
                    break
                self.phase2(l)
                if self.upto == "p2":
                    break
                self.phase3(l, last)
                if self.upto == "p3":
                    break
                self.phase4(l, last)
                if self.upto in ("p4", "p4s"):
                    break
                self.phase56(l, last)
                if self.upto == "p56":
                    break
            if self.upto is None:
                self.epilogue()
            p.finish("sp")
            p.emit(st)
        return nc

    def prologue(self):
        nc, p, ar, I = self.nc, self.p, self.arena, self.i
        self.ident = ar.alloc([128, 128], F32, "ident")
        p.dma("sp", self.ident[:], I["c_ident"], w=[("ident",)])
        self.ones_f = ar.alloc([128, 128], F32, "ones_f")
        self.ones_bf = ar.alloc([128, 128], BF16, "ones_bf")
        self.eps_t = ar.alloc([128, 1], F32, "eps")
        p.op("dve", lambda e: e.memset(self.ones_f[:], 1.0), w=[("ones_f",)])
        p.op("dve", lambda e: e.memset(self.ones_bf[:], 1.0), w=[("ones_bf",)])
        p.op("dve", lambda e: e.memset(self.eps_t[:], EPS), w=[("eps",)])
        self.eps_ap = self.eps_t[:, 0:1]
        self.lrt_tmp = ar.alloc([128, 128], F32, "lrt_tmp")
        self.n1gT = ar.alloc([128, L, 8], F32, "n1gT")
        self.n2gT = ar.alloc([128, L, 8], F32, "n2gT")
        self.fngT = ar.alloc([128, 8], F32, "fngT")
        self.badaT = ar.alloc([128, L, 48], F32, "badaT")
        self.qagT = ar.alloc([128, L, 2], F32, "qagT")
        self.kvagT = ar.alloc([128, L], F32, "kvagT")
        self.gngT = ar.alloc([128, L], F32, "gngT")
        self.convwT = ar.alloc([128, L, 5, 12], F32, "convwT")
        self.ccT = ar.alloc([128, 2, 8], F32, "ccT")
        self.scT = ar.alloc([128, 2, 8], F32, "scT")
        self.modT = ar.alloc([128, L, 48, 2], F32, "modT")
        self.A1 = ar.alloc([128, L, 8, 2], F32, "A1")
        self.A2 = ar.alloc([128, L, 8, 2], F32, "A2")
        self.alog = ar.alloc([128, L * 8], F32, "alog")
        self.negA = ar.alloc([128, L * 8], F32, "negA")
        self.dtb = ar.alloc([128, L * 8], F32, "dtb")
        fl = lambda t: t[:].rearrange("p a b -> p (a b)") if len(t.shape) == 3 else t[:]
        self.load_rows_T(I["norm1_g"].rearrange("l (c q) -> (l c) q", q=128), 32, fl(self.n1gT), ("n1gT",))
        self.load_rows_T(I["norm2_g"].rearrange("l (c q) -> (l c) q", q=128), 32, fl(self.n2gT), ("n2gT",))
        self.load_rows_T(I["final_norm_g"].rearrange("l (c q) -> (l c) q", q=128), 8, self.fngT[:], ("fngT",))
        bview = I["b_ada"].rearrange("l (f q) -> (l f) q", q=128)
        bflat = fl(self.badaT)
        for h in range(2):
            self.load_rows_T(bview[h * 96:(h + 1) * 96, :], 96, bflat[:, h * 96:(h + 1) * 96], ("badaT", h))
        self.load_rows_T(I["q_a_g"].rearrange("l (c q) -> (l c) q", q=128), 8, fl(self.qagT), ("qagT",))
        self.load_rows_T(I["kv_a_g"], 4, self.kvagT[:], ("kvagT",))
        self.load_rows_T(I["gdn_norm_g"], 4, self.gngT[:], ("gngT",))
        cview = I["conv_w"].rearrange("l j (c q) -> (l j c) q", q=128)
        cflat = self.convwT[:].rearrange("p l j c -> p (l j c)")
        for h in range(2):
            self.load_rows_T(cview[h * 120:(h + 1) * 120, :], 120, cflat[:, h * 120:(h + 1) * 120], ("convwT", h))
        self.load_rows_T(I["cc"].rearrange("k (c q) -> (k c) q", q=128), 16, fl(self.ccT), ("ccT",))
        self.act(fl(self.scT), fl(self.ccT), AF.Silu, r=[("ccT",)], w=[("scT",)])
        p.dma("sp", self.alog[:], I["a_log"].rearrange("l e -> (l e)").partition_broadcast(128), w=[("alog",)])
        p.dma("sp", self.dtb[:], I["dt_bias"].rearrange("l e -> (l e)").partition_broadcast(128), w=[("dtb",)])
        self.act(self.negA[:], self.alog[:], AF.Exp, r=[("alog",)], w=[("negA",)])
        self.ts("dve", self.negA[:], self.negA[:], -1.0, ALU.mult, r=[("negA",)], w=[("negA",)])

        m0 = ar.mark()
        XL = [ar.alloc([128, DM], F32, "XL") for _ in range(2)]
        XO = [ar.alloc([128, 8, 128], F32, "XO") for _ in range(2)]
        xTv = self.xT.rearrange("(c q) t -> q c t", q=128)
        for i in range(T // 128):
            b = i % 2
            src = I["ctx"][i * 128:(i + 1) * 128, :] if i < 2 else I["x"][(i - 2) * 128:(i - 1) * 128, :]
            p.dma("sp", XL[b][:], src, w=[("XL", b)])
            for hb in range(2):
                ps, pk = self.bank()
                for j in range(4):
                    c = hb * 4 + j
                    self.tr(ps[:, j * 128:(j + 1) * 128], XL[b][:, c * 128:(c + 1) * 128], self.ident[:],
                            r=[("XL", b), ("ident",)], w=[pk])
                self.cp("dve" if hb == 0 else "act", XO[b][:, hb * 4:(hb + 1) * 4, :],
                        ps[:].rearrange("q (j t) -> q j t", j=4), r=[pk], w=[("XO", b, hb)])
            p.dma("sp", xTv[:, :, i * 128:(i + 1) * 128], XO[b][:], r=[("XO", b, 0), ("XO", b, 1)], w=[("xT", i)])
        WA = [ar.alloc([128, 8, 512], F32, "WA") for _ in range(2)]
        nb = 0
        for l in range(self.nlayers):
            ps, pk = self.bank()
            for fb in range(12):
                b = nb % 2
                nb += 1
                p.dma("sp" if fb % 2 == 0 else "act", WA[b][:],
                      I["w_ada"][l, :, fb * 512:(fb + 1) * 512].rearrange("(k q) f -> q k f", q=128), w=[("WA", b)])
                for j in range(4):
                    fc = fb * 4 + j
                    for k in range(8):
                        self.mm(ps[:, fc * 2:fc * 2 + 2], WA[b][:, k, j * 128:(j + 1) * 128], self.scT[:, :, k],
                                start=(k == 0), stop=(k == 7), r=[("WA", b), ("scT",)], w=[pk])
            self.tt("dve", self.modT[:, l, :, :], ps[:, 0:96].rearrange("q (f k) -> q f k", k=2),
                    self.badaT[:, l, :].unsqueeze(2).broadcast_to([128, 48, 2]), ALU.add,
                    r=[pk, ("badaT", 0), ("badaT", 1)], w=[("modT", l)])
            for (A, g, o) in ((self.A1, self.n1gT, 8), (self.A2, self.n2gT, 32)):
                self.stt(A[:, l, :, :], self.modT[:, l, o:o + 8, :], 1.0,
                         g[:, l, :].unsqueeze(2).broadcast_to([128, 8, 2]), ALU.add, ALU.mult,
                         r=[("modT", l), ("n1gT",), ("n2gT",)], w=[("A", l, o)])
        if self.debug:
            p.dma("sp", self.dbg_mod[:, 0:self.nlayers * 96],
                  self.modT[:, 0:self.nlayers, :, :].rearrange("q l f k -> q (l f k)"),
                  r=[("modT", l) for l in range(self.nlayers)])
        ar.reset(m0)
        self.m_phase = m0
        p.barrier()

    def modv(self, l, which, c, col):
        return self.modT[:, l, which * 8 + c, col:col + 1]

    def load_weight_bf16(self, dst_ap, src_ap, wkey, stg):
        p = self.p
        K_, F_ = dst_ap.shape[1], dst_ap.shape[2]
        for k in range(K_):
            for f0 in range(0, F_, 2048):
                fw = min(2048, F_ - f0)
                i = self.stg_i
                self.stg_i += 1
                sb = i % len(stg)
                p.dma("sp" if i % 2 == 0 else "act", stg[sb][:, 0:fw], src_ap[:, k, f0:f0 + fw], w=[("stg", sb)])
                eng = ("dve", "act")[i % 2]
                self.cp(eng, dst_ap[:, k, f0:f0 + fw], stg[sb][:, 0:fw], r=[("stg", sb)], w=[(wkey[0], i)])
                self.wkeys.setdefault(wkey[0], []).append((wkey[0], i))

    def phase1(self, l):
        nc, p, ar, I = self.nc, self.p, self.arena, self.i
        ar.reset(self.m_phase)
        self.banks, self.pi = list(range(8)), 0
        Win = ar.alloc([128, 8, D_IN], BF16, "Win")
        WinR = ar.alloc([128, 8, 32], BF16, "WinR")
        Wqb = ar.alloc([128, 2, 8, 96], BF16, "Wqb")
        WqbR = ar.alloc([128, 2, 8, 32], BF16, "WqbR")
        Wkvb = ar.alloc([128, 8, 128], BF16, "Wkvb")
        mW = ar.mark()
        stg = [ar.alloc([128, 2048], F32, "stg") for _ in range(3)]
        self.wkeys, self.stg_i = {}, 0
        self.load_weight_bf16(Win[:], I["w_in"][l].rearrange("(k q) f -> q k f", q=128), ("Win",), stg)
        self.load_weight_bf16(Wqb[:].rearrange("q k h e -> q k (h e)"), I["w_q_b"][l].rearrange("(k q) f -> q k f", q=128), ("Wqb",), stg)
        self.load_weight_bf16(Wkvb[:].rearrange("q h e -> q (h e)").unsqueeze(1), I["w_kv_b"][l].unsqueeze(1), ("Wkvb",), stg)
        p.barrier()
        ar.reset(mW)
        self.ts("dve", WinR[:, :, 0:16], Win[:, :, 400:416], -1.0, ALU.mult, r=[("Win",)], w=[("WinR", 0)])
        self.cp("dve", WinR[:, :, 16:32], Win[:, :, 384:400], r=[("Win",)], w=[("WinR", 1)])
        self.ts("dve", WqbR[:, :, :, 0:16], Wqb[:, :, :, 80:96], -1.0, ALU.mult, r=[("Wqb",)], w=[("WqbR", 0)])
        self.cp("dve", WqbR[:, :, :, 16:32], Wqb[:, :, :, 64:80], r=[("Wqb",)], w=[("WqbR", 1)])
        rWinR = [("WinR", 0), ("WinR", 1)]
        rWqbR = [("WqbR", 0), ("WqbR", 1)]

        STOP = int(os.environ.get("P1_STOP", "99"))
        if STOP == 0:
            p.barrier()
            return
        X = [ar.alloc([128, 8, 512], F32, "X") for _ in range(2)]
        SQ = ar.alloc([128, 8, 512], BF16, "SQ")
        HT = ar.alloc([128, 8, 512], BF16, "HT")
        T1 = [ar.alloc([128, 512], F32, "T1") for _ in range(2)]
        LN = ar.alloc([128, 512], F32, "LN")
        RS = ar.alloc([128, 512], F32, "RS")
        RSQ = ar.alloc([128, 512], F32, "RSQ")
        QA = ar.alloc([128, 3, 512], F32, "QA")
        SQ2 = ar.alloc([128, 3, 512], BF16, "SQ2")
        QAN = ar.alloc([128, 3, 512], BF16, "QAN")
        PQ = ar.alloc([128, 12, 512], BF16, "PQ")
        ZSt = ar.alloc([128, 4, 512], BF16, "ZSt")
        ZE = [ar.alloc([128, 512], F32, "ZE") for _ in range(2)]
        KTt = ar.alloc([128, 8, 512], BF16, "KTt")
        QTt = ar.alloc([128, 8, 512], BF16, "QTt")
        VT = ar.alloc([128, 4, 512], BF16, "VT")
        CT = ar.alloc([128, 512], F32, "CT")
        STb = ar.alloc([128, 512], F32, "STb")
        R1 = [ar.alloc([128, 512], F32, "R1") for _ in range(2)]
        R2 = [ar.alloc([128, 512], F32, "R2") for _ in range(2)]
        KPE = ar.alloc([128, 512], BF16, "KPE")
        GBt = ar.alloc([128, 4, 16], F32, "GBt")
        GX = ar.alloc([128, 4, 16], F32, "GX")

        xTv = self.xT.rearrange("(c q) t -> q c t", q=128)
        tiles = [(0, 256, 0)] + [(256 + 512 * i, 512, 1) for i in range(8)]
        ones_bf = self.ones_bf
        for ti, (t0, n, col) in enumerate(tiles):
            xb = ti % 2
            Xt = X[xb]
            p.dma("sp", Xt[:, :, 0:n], xTv[:, :, t0:t0 + n], r=[("xT_l", l - 1)], w=[("X", xb)])
            p.dma("act", CT[64:96, 0:n], I["c_ropec"][:, t0:t0 + n], w=[("CT",)])
            p.dma("act", STb[64:96, 0:n], I["c_ropes"][:, t0:t0 + n], w=[("STb",)])
            self.tt("dve", SQ[:, :, 0:n], Xt[:, :, 0:n], Xt[:, :, 0:n], ALU.mult, r=[("X", xb)], w=[("SQ",)])
            ps, pk = self.bank()
            for k in range(8):
                self.mm(ps[:, 0:n], ones_bf[:], SQ[:, k, 0:n], start=(k == 0), stop=(k == 7), r=[("SQ",), ("ones_bf",)], w=[pk])
            self.rsqrt_from_psum(RS[:, 0:n], ps[:, 0:n], float(DM), n, r=[pk], w=[("RS",)], tmp=LN[:, 0:n])
            for k in range(8):
                tb = k % 2
                self.stt(T1[tb][:, 0:n], Xt[:, k, 0:n], self.A1[:, l, k, col:col + 1], RS[:, 0:n], ALU.mult, ALU.mult,
                         r=[("X", xb), ("RS",), ("A", l, 8)], w=[("T1", tb)])
                self.act(HT[:, k, 0:n], T1[tb][:, 0:n], AF.Identity, bias=self.modv(l, 0, k, col),
                         r=[("T1", tb), ("modT", l)], w=[("HT", k)])
            rHT = [("HT", k) for k in range(8)]
            if STOP == 1:
                continue

            def proj(c0, m, out_ap, lhs=None, rot=False):
                for k in range(8):
                    lt = WinR[:, k, :] if rot else Win[:, k, c0:c0 + m]
                    self.mm(out_ap, lt, HT[:, k, 0:n], start=(k == 0), stop=(k == 7),
                            r=rHT + [("Win",)] + (rWinR if rot else []), w=[out_ap_key[0]])

            for j in range(3):
                ps, pk = self.bank()
                out_ap_key = [pk]
                proj(j * 128, 128, ps[:, 0:n])
                self.cp("dve", QA[:, j, 0:n], ps[:, 0:n], r=[pk], w=[("QA", j)])
                self.tt("dve", SQ2[:, j, 0:n], QA[:, j, 0:n], QA[:, j, 0:n], ALU.mult, r=[("QA", j)], w=[("SQ2", j)])
            ps, pk = self.bank()
            for j in range(2):
                self.mm(ps[:, 0:n], ones_bf[:], SQ2[:, j, 0:n], start=(j == 0), stop=(j == 1), r=[("SQ2", j), ("ones_bf",)], w=[pk])
            self.rsqrt_from_psum(RSQ[:, 0:n], ps[:, 0:n], 256.0, n, r=[pk], w=[("RSQ",)], tmp=LN[:, 0:n])
            for j in range(2):
                self.stt(QAN[:, j, 0:n], QA[:, j, 0:n], self.qagT[:, l, j:j + 1], RSQ[:, 0:n], ALU.mult, ALU.mult,
                         r=[("QA", j), ("RSQ",), ("qagT",)], w=[("QAN", j)])
            ps, pk = self.bank()
            self.mm(ps[:, 0:n], ones_bf[:], SQ2[:, 2, 0:n], r=[("SQ2", 2), ("ones_bf",)], w=[pk])
            self.rsqrt_from_psum(RSQ[:, 0:n], ps[:, 0:n], 128.0, n, r=[pk], w=[("RSQ",)], tmp=LN[:, 0:n])
            self.stt(QAN[:, 2, 0:n], QA[:, 2, 0:n], self.kvagT[:, l:l + 1], RSQ[:, 0:n], ALU.mult, ALU.mult,
                     r=[("QA", 2), ("RSQ",), ("kvagT",)], w=[("QAN", 2)])
            if STOP == 2:
                continue
            psA, pkA = self.bank()
            out_ap_key = [pkA]
            proj(384, 32, psA[64:96, 0:n])
            psB, pkB = self.bank()
            out_ap_key = [pkB]
            proj(0, 32, psB[64:96, 0:n], rot=True)
            self.tt("dve", R1[0][64:96, 0:n], psA[64:96, 0:n], CT[64:96, 0:n], ALU.mult, r=[pkA, ("CT",)], w=[("R1", 0)])
            self.tt("dve", R2[0][64:96, 0:n], psB[64:96, 0:n], STb[64:96, 0:n], ALU.mult, r=[pkB, ("STb",)], w=[("R2", 0)])
            self.tt("dve", KPE[64:96, 0:n], R1[0][64:96, 0:n], R2[0][64:96, 0:n], ALU.add, r=[("R1", 0), ("R2", 0)], w=[("KPE",)])
            self.cp("act", KTt[64:96, :, 0:n], KPE[64:96, 0:n].unsqueeze(1).broadcast_to([32, 8, n]),
                    r=[("KPE",)], w=[("KTt", "pe")])
            if STOP == 3:
                continue
            for j in range(12):
                ps, pk = self.bank()
                out_ap_key = [pk]
                proj(416 + 128 * j, 128, ps[:, 0:n])
                self.cp("act" if j % 2 == 0 else "dve", PQ[:, j, 0:n], ps[:, 0:n], r=[pk], w=[("PQ", j)])
            for hh in range(2):
                p.dma("sp", self.pqkv.rearrange("(c q) t -> q c t", q=128)[:, hh * 6:(hh + 1) * 6, t0:t0 + n], PQ[:, hh * 6:(hh + 1) * 6, 0:n],
                      r=[("PQ", j) for j in range(hh * 6, (hh + 1) * 6)], w=[("pqkv", ti, hh)])
            for j in range(4):
                ps, pk = self.bank()
                out_ap_key = [pk]
                proj(1952 + 128 * j, 128, ps[:, 0:n])
                self.act(ZSt[:, j, 0:n], ps[:, 0:n], AF.Silu, r=[pk], w=[("ZSt", j)])
            p.dma("sp", self.zs.rearrange("(c q) t -> q c t", q=128)[:, :, t0:t0 + n], ZSt[:, :, 0:n],
                  r=[("ZSt", j) for j in range(4)], w=[("zs", ti)])
            if STOP == 4:
                continue
            nj = n // 128
            for j in range(nj):
                ps, pk = self.bank()
                for k in range(8):
                    self.mm(ps[:, 0:16], HT[:, k, j * 128:(j + 1) * 128], Win[:, k, 2464:2480], start=(k == 0), stop=(k == 7),
                            r=rHT + [("Win",)], w=[pk])
                self.tt("dve", GX[:, j, 0:8], ps[:, 0:8], self.dtb[:, l * 8:(l + 1) * 8], ALU.add, r=[pk, ("dtb",)], w=[("GX", j)])
                self.act(GX[:, j, 0:8], GX[:, j, 0:8], AF.Exp, r=[("GX", j)], w=[("GX", j)])
                self.cp("dve", GX[:, j, 8:16], ps[:, 8:16], r=[pk, ("GX", j)], w=[("GX", j)])
                self.act(GX[:, j, 8:16], GX[:, j, 8:16], AF.Exp, scale=-1.0, r=[("GX", j)], w=[("GX", j)])
                self.act(GX[:, j, 0:8], GX[:, j, 0:8], AF.Ln, bias=1.0, r=[("GX", j)], w=[("GX", j)])
                self.tt("dve", GBt[:, j, 0:8], GX[:, j, 0:8], self.negA[:, l * 8:(l + 1) * 8], ALU.mult, r=[("GX", j), ("negA",)], w=[("GBt", j)])
                self.ts("dve", GX[:, j, 8:16], GX[:, j, 8:16], 1.0, ALU.add, r=[("GX", j)], w=[("GX", j)])
                p.op("dve", lambda e, o=GBt[:, j, 8:16], i_=GX[:, j, 8:16]: e.reciprocal(out=o, in_=i_), r=[("GX", j), ("GBt", j)], w=[("GBt", j)])
            p.dma("sp", self.gb[t0:t0 + n, :].rearrange("(j q) f -> q j f", q=128), GBt[:, 0:nj, :],
                  r=[("GBt", j) for j in range(nj)], w=[("gb", ti)])
            if STOP == 5:
                continue
            for h in range(8):
                psA, pkA = self.bank()
                for kk in range(2):
                    self.mm(psA[0:96, 0:n], Wqb[:, kk, h, :], QAN[:, kk, 0:n], start=(kk == 0), stop=(kk == 1),
                            r=[("Wqb",), ("QAN", 0), ("QAN", 1)], w=[pkA])
                psB, pkB = self.bank()
                for kk in range(2):
                    self.mm(psB[64:96, 0:n], WqbR[:, kk, h, :], QAN[:, kk, 0:n], start=(kk == 0), stop=(kk == 1),
                            r=rWqbR + [("QAN", 0), ("QAN", 1)], w=[pkB])
                rb = h % 2
                self.cp("dve", QTt[0:64, h, 0:n], psA[0:64, 0:n], r=[pkA], w=[("QTt", h, 0)])
                self.tt("dve", R1[rb][64:96, 0:n], psA[64:96, 0:n], CT[64:96, 0:n], ALU.mult, r=[pkA, ("CT",)], w=[("R1", rb)])
                self.tt("dve", R2[rb][64:96, 0:n], psB[64:96, 0:n], STb[64:96, 0:n], ALU.mult, r=[pkB, ("STb",)], w=[("R2", rb)])
                self.tt("dve", QTt[64:96, h, 0:n], R1[rb][64:96, 0:n], R2[rb][64:96, 0:n], ALU.add,
                        r=[("R1", rb), ("R2", rb)], w=[("QTt", h, 1)])
            p.dma("sp", self.qT[:, :, t0:t0 + n].rearrange("h d t -> d h t"), QTt[0:96, :, 0:n],
                  r=[("QTt", h, s) for h in range(8) for s in range(2)], w=[("qT", ti)])
            if STOP == 6:
                continue
            for h in range(8):
                ps, pk = self.bank()
                self.mm(ps[0:64, 0:n], Wkvb[:, h, 0:64], QAN[:, 2, 0:n], r=[("Wkvb",), ("QAN", 2)], w=[pk])
                self.cp("act" if h % 2 == 0 else "dve", KTt[0:64, h, 0:n], ps[0:64, 0:n], r=[pk], w=[("KTt", h)])
            p.dma("sp", self.kT[:, :, t0:t0 + n].rearrange("h d t -> d h t"), KTt[0:96, :, 0:n],
                  r=[("KTt", h) for h in range(8)] + [("KTt", "pe")], w=[("kT", ti)])
            for j in range(nj):
                ps, pk = self.bank()
                self.mm(ps[:, :].rearrange("q (h e) -> q h e", e=64), QAN[:, 2, j * 128:(j + 1) * 128], Wkvb[:, :, 64:128],
                        r=[("Wkvb",), ("QAN", 2)], w=[pk])
                self.cp("act" if j % 2 == 0 else "dve", VT[:, j, :], ps[:, :], r=[pk], w=[("VT", j)])
            p.dma("sp", self.vtok[t0:t0 + n].rearrange("(j q) h e -> q j (h e)", q=128), VT[:, 0:nj, :],
                  r=[("VT", j) for j in range(nj)], w=[("vtok", ti)])
        p.barrier()

    def phase2(self, l):
        nc, p, ar, I = self.nc, self.p, self.arena, self.i
        ar.reset(self.m_phase)
        self.banks, self.pi = list(range(8)), 0
        Dg = ar.alloc([128, 12, 5, 128], BF16, "Dg")
        for c in range(12):
            for j in range(5):
                self.ts("dve", Dg[:, c, j, :], self.ident[:], self.convwT[:, l, j, c:c + 1], ALU.mult,
                        r=[("ident",), ("convwT", 0), ("convwT", 1)], w=[("Dg", c, j)])
        PT = [ar.alloc([128, 12, 516], BF16, "PT") for _ in range(2)]
        QC = [ar.alloc([128, 12, 512], F32, "QC") for _ in range(2)]
        SQ = [ar.alloc([128, 512], BF16, "SQ") for _ in range(2)]
        LN8 = ar.alloc([128, 8, 512], F32, "LN8")
        pv = self.pqkv.rearrange("(c q) t -> q c t", q=128)
        qv = self.qkvc.rearrange("(c q) t -> q c t", q=128)
        tiles = [(0, 256, 0, T_C)] + [(256 + 512 * i, 512, T_C, T) for i in range(8)]
        for ti, (t0, n, s0, s1) in enumerate(tiles):
            b = ti % 2
            lo, hi = t0 - 2, t0 + n + 2
            vlo, vhi = max(lo, s0), min(hi, s1)
            if lo < s0:
                p.op("dve", lambda e, a=PT[b][:, :, 0:2]: e.memset(a, 0.0), w=[("PT", b, 0), ("PT", b, 1)])
            if hi > s1:
                p.op("dve", lambda e, a=PT[b][:, :, n + 2:n + 4]: e.memset(a, 0.0), w=[("PT", b, 0), ("PT", b, 1)])
            for hh in range(2):
                p.dma("sp" if hh == 0 else "act", PT[b][:, hh * 6:(hh + 1) * 6, vlo - lo:vhi - lo], pv[:, hh * 6:(hh + 1) * 6, vlo:vhi],
                      r=[("PT", b, hh)], w=[("PT", b, hh)])
            for c in range(12):
                ps, pk = self.bank()
                for j in range(5):
                    self.mm(ps[:, 0:n], Dg[:, c, j, :], PT[b][:, c, j:j + n], start=(j == 0), stop=(j == 4),
                            r=[("PT", b, c // 6), ("Dg", c, j)], w=[pk])
                self.act(QC[b][:, c, 0:n], ps[:, 0:n], AF.Silu, r=[pk], w=[("QC", b, c)])
            for c in range(8):
                eb = c % 2
                self.tt("dve", SQ[eb][:, 0:n], QC[b][:, c, 0:n], QC[b][:, c, 0:n], ALU.mult, r=[("QC", b, c)], w=[("SQ", eb)])
                ps2, pk2 = self.bank()
                self.mm(ps2[:, 0:n], self.ones_bf[:], SQ[eb][:, 0:n], r=[("SQ", eb), ("ones_bf",)], w=[pk2])
                self.act(LN8[:, c, 0:n], ps2[:, 0:n], AF.Ln, bias=self.eps_ap, r=[pk2, ("eps",)], w=[("LN8", c)])
            self.act(LN8[:, :, 0:n], LN8[:, :, 0:n], AF.Exp, scale=-0.5, r=[("LN8", c) for c in range(8)], w=[("LN8", c) for c in range(8)])
            for c in range(8):
                self.stt(QC[b][:, c, 0:n], QC[b][:, c, 0:n], (128.0 ** -0.5) if c < 4 else 1.0, LN8[:, c, 0:n], ALU.mult, ALU.mult,
                         r=[("QC", b, c), ("LN8", c)], w=[("QC", b, c)])
            for hh in range(2):
                p.dma("sp", qv[:, hh * 6:(hh + 1) * 6, t0:t0 + n], QC[b][:, hh * 6:(hh + 1) * 6, 0:n],
                      r=[("QC", b, c) for c in range(hh * 6, (hh + 1) * 6)], w=[("qkvc", ti, hh)])
        p.barrier()

    def phase3(self, l, last):
        nc, p, ar, I = self.nc, self.p, self.arena, self.i
        ar.reset(self.m_phase)
        KTh = [ar.alloc([128, T], BF16, "KTh") for _ in range(2)]
        QTh = [ar.alloc([128, T], BF16, "QTh") for _ in range(2)]
        NKT = T // 128
        Vall = ar.alloc([128, NKT, 512], BF16, "Vall")
        Vaug = ar.alloc([128, NKT, 8, 128], BF16, "Vaug")
        PTt = [ar.alloc([128, 512], BF16, "PTt") for _ in range(6)]
        Rr = [ar.alloc([128, 512], F32, "Rr") for _ in range(2)]
        AT = [ar.alloc([128, 512], BF16, "AT") for _ in range(2)]
        vv = self.vtok.rearrange("(j q) h e -> q j (h e)", q=128)
        for g_ in range(0, NKT, 4):
            ge = min(NKT, g_ + 4)
            p.dma("sp" if (g_ // 4) % 2 == 0 else "act", Vall[:, g_:ge, :], vv[:, g_:ge, :], w=[("Vall", g_)])
            p.op("dve", lambda e, a=Vaug[:, g_:ge, :, 64:128]: e.memset(a, 1.0), w=[("Vaug1", g_)])
            self.cp("act" if (g_ // 4) % 2 == 0 else "dve", Vaug[:, g_:ge, :, 0:64], Vall[:, g_:ge, :].rearrange("q j (h e) -> q j h e", e=64),
                    r=[("Vall", g_)], w=[("Vaug0", g_)])
        rV = [("Vaug0", g_) for g_ in range(0, NKT, 4)] + [("Vaug1", g_) for g_ in range(0, NKT, 4)]
        scale = 96.0 ** -0.5
        qtiles = ([] if last else [(0, 256, 2)]) + [(256 + 512 * i, 512, T // 128) for i in range(8)]
        sbanks, obanks = [0, 1, 2, 3, 6, 7], [4, 5]
        si = oi = pi_ = 0
        for h in range(8):
            hb = h % 2
            p.dma("sp", KTh[hb][0:96, :], self.kT[h], w=[("KTh", hb)])
            p.dma("act", QTh[hb][0:96, :], self.qT[h], w=[("QTh", hb)])
            seq = [(qi, kt) for qi, (t0, n, nk) in enumerate(qtiles) for kt in range(nk)]
            sinfo = {}

            def emit_S(i):
                nonlocal si
                qi, kt = seq[i]
                t0, n, nk = qtiles[qi]
                b = sbanks[si % len(sbanks)]
                si += 1
                self.mm(self.PS[b][:, 0:n], KTh[hb][0:96, kt * 128:(kt + 1) * 128], QTh[hb][0:96, t0:t0 + n],
                        r=[("KTh", hb), ("QTh", hb)], w=[("ps", b)])
                sinfo[i] = b

            LOOK = 5
            for i in range(min(LOOK, len(seq))):
                emit_S(i)
            ob = None
            for i, (qi, kt) in enumerate(seq):
                t0, n, nk = qtiles[qi]
                if kt == 0:
                    ob = obanks[oi % 2]
                    oi += 1
                b = sinfo.pop(i)
                pt = pi_ % 6
                pi_ += 1
                self.act(PTt[pt][:, 0:n], self.PS[b][:, 0:n], AF.Exp, scale=scale, r=[("ps", b)], w=[("PTt", pt)])
                if i + LOOK < len(seq):
                    emit_S(i + LOOK)
                self.mm(self.PS[ob][:, 0:n], Vaug[:, kt, h, :], PTt[pt][:, 0:n], start=(kt == 0), stop=(kt == nk - 1),
                        r=rV + [("PTt", pt)], w=[("ps", ob)])
                if kt == nk - 1:
                    rb = oi % 2
                    p.op("dve", lambda e, o=Rr[rb][0:64, 0:n], i_=self.PS[ob][64:128, 0:n]: e.reciprocal(out=o, in_=i_),
                         r=[("ps", ob)], w=[("Rr", rb)])
                    self.tt("dve", AT[rb][0:64, 0:n], self.PS[ob][0:64, 0:n], Rr[rb][0:64, 0:n], ALU.mult,
                            r=[("ps", ob), ("Rr", rb)], w=[("AT", rb)])
                    p.dma("sp", self.mixT[h * 64:(h + 1) * 64, t0:t0 + n], AT[rb][0:64, 0:n], r=[("AT", rb)], w=[("mixT", h, qi)])
        p.barrier()

    def phase4(self, l, last):
        nc, p, ar, I = self.nc, self.p, self.arena, self.i
        ar.reset(self.m_phase)
        self.banks, self.pi = list(range(8)), 0
        self.bgrp = {0: [[0, 1, 2], 0], 1: [[3, 4, 5], 0], "s": [[6, 7], 0]}
        GM = ar.alloc([64, 10, 64], F32, "GM")
        p.dma("sp", GM[:], I["c_gmask"], w=[("GM",)])
        GBall = ar.alloc([64, NCH, 16], F32, "GBall")
        gbv = self.gb.rearrange("(n q) f -> q n f", q=64)
        for g_ in range(0, NCH, 8):
            ge = min(NCH, g_ + 8)
            p.dma("sp", GBall[:, g_:ge, :], gbv[:, g_:ge, :], w=[("GBall", g_)])
        p.op("dve", lambda e: e.nop(), r=[("GBall", g_) for g_ in range(0, NCH, 8)], w=[("GBall",)])
        OA = ar.alloc([128, 4, T], F32, "OA")
        for h in range(4):
            p.op("dve", lambda e, a=OA[:, h, :]: e.memset(a, 0.0), w=[("OA", n) for n in range(NCH)] if h == 0 else [])
        S = [ar.alloc([128, 4, 128], F32, "S") for _ in range(2)]
        for d in range(2):
            p.op("dve", lambda e, a=S[d][:]: e.memset(a, 0.0), w=[("S", d)])
        m_scan = ar.mark()
        QKt = [[ar.alloc([128, 12, 128], F32, "QKt") for _ in range(2)] for _ in range(2)]
        I64 = GM[:, 8, :]
        NEG4 = [ar.alloc([64, 4, 64], F32, "NEG4") for _ in range(2)]
        for d in range(2):
            self.cp("dve", NEG4[d][:], GM[:, 4 + d, :].unsqueeze(1).broadcast_to([64, 4, 64]), r=[("GM",)], w=[("NEG4",)])
        A64 = lambda sh: I64.unsqueeze(1).broadcast_to(sh)
        Bi, Bo = [], []
        for d in range(2):
            bi = dict(KV=ar.alloc([64, 8, 128], F32, "KV"), sm=ar.alloc([64, 8, 4], F32, "sm"),
                      GB2=ar.alloc([64, 4, 64], F32, "GB2"), DEC=ar.alloc([64, 4, 64], F32, "DEC"),
                      QKD=ar.alloc([64, 4, 64], F32, "QKD"), DS=ar.alloc([64, 4, 64], F32, "DS"),
                      M0=ar.alloc([64, 4, 64], F32, "M0"), N0=ar.alloc([64, 4, 64], F32, "N0"),
                      M=[ar.alloc([64, 4, 64], F32, "Mx") for _ in range(2)],
                      NP=[ar.alloc([64, 4, 128], F32, "NPx") for _ in range(2)],
                      DG2=ar.alloc([64, 4, 64], F32, "DG2"),
                      PRE=ar.alloc([64, 4, 128], F32, "PRE"), VN=ar.alloc([64, 4, 128], F32, "VN"))
            bo = [dict(PT5=ar.alloc([64, 4, 64], F32, "PT5"), qkT=ar.alloc([64, 4, 64], F32, "qkT"),
                       qdT=ar.alloc([128, 4, 64], F32, "qdT"), vb=ar.alloc([64, 4, 128], F32, "vb"),
                       ktl=ar.alloc([64, 4, 128], F32, "ktl"), kTc=ar.alloc([128, 4, 64], F32, "kTc"),
                       nbe=ar.alloc([64, 4], F32, "nbe"), gl=ar.alloc([128, 4], F32, "gl")) for _ in range(2)]
            Bi.append(bi)
            Bo.append(bo)
        P4S = int(os.environ.get("P4_STOP", "99"))
        order = [list(range(NCH)), list(range(NCC - 1, -1, -1)) + list(range(NCH - 1, NCC - 1, -1))]
        qv = self.qkvc.rearrange("(c q) t -> q c t", q=128)
        tile_buf = [dict(), dict()]
        nload = [0, 0]

        def ensure_tile(d, s):
            if s >= NCH:
                return
            tt_ = order[d][s] // 2
            if tt_ in tile_buf[d]:
                return
            b = nload[d] % 2
            nload[d] += 1
            for k_ in [k_ for k_, v_ in tile_buf[d].items() if v_ == b]:
                del tile_buf[d][k_]
            tile_buf[d][tt_] = b
            for hh in range(2):
                p.dma("sp" if d == 0 else "act", QKt[d][b][:, hh * 6:(hh + 1) * 6, :], qv[:, hh * 6:(hh + 1) * 6, tt_ * 128:(tt_ + 1) * 128],
                      w=[("QKt", d, b, hh)])

        def prepass(d, s):
            n = order[d][s]
            pb = s % 2
            bi, bo = Bi[d], Bo[d][pb]
            tb = tile_buf[d][n // 2]
            QK = QKt[d][tb]
            rQK = [("QKt", d, tb, 0), ("QKt", d, tb, 1)]
            c0 = (n % 2) * 64
            qT, kT, vT = QK[:, 0:4, c0:c0 + 64], QK[:, 4:8, c0:c0 + 64], QK[:, 8:12, c0:c0 + 64]
            A_d, B_d, NEGI, NSTR = GM[:, d, :], GM[:, 2 + d, :], GM[:, 4 + d, :], GM[:, 6 + d, :]
            g = GBall[:, n, d * 4:(d + 1) * 4]
            beta = GBall[:, n, 8 + d * 4:12 + d * 4]
            rG = [("GBall",), ("GM",)]
            ki = lambda nm: ("gi", d, nm)
            ko = lambda nm: ("go", d, pb, nm)
            sm = bi["sm"]
            gcs, e, dlt, ft = sm[:, 0, :], sm[:, 1, :], sm[:, 2, :], sm[:, 3, :]
            ps1, pk1 = self.bank(d)
            ps2, pk2 = self.bank(d)
            for h in range(4):
                self.tr(ps1[0:64, h * 128:(h + 1) * 128], kT[:, h, :], self.ident[:], r=rQK + [("ident",)], w=[pk1])
            for h in range(4):
                self.tr(ps2[0:64, h * 128:(h + 1) * 128], vT[:, h, :], self.ident[:], r=rQK + [("ident",)], w=[pk2])
            ps3, pk3 = self.bank(d)
            self.mm(ps3[0:64, 0:4], A_d, g, r=rG, w=[pk3])
            self.mm(ps3[:, 8:12], self.ones_f[0:64, :], g, r=rG + [("ones_f",)], w=[pk3])
            self.cp("dve", bi["KV"][:, 0:4, :], ps1[0:64, :].rearrange("q (h e) -> q h e", e=128), r=[pk1], w=[ki("K")])
            self.cp("act", bi["KV"][:, 4:8, :], ps2[0:64, :].rearrange("q (h e) -> q h e", e=128), r=[pk2], w=[ki("V")])
            yield
            if P4S <= 1:
                return
            self.cp("dve", gcs, ps3[0:64, 0:4], r=[pk3], w=[ki("gcs")])
            self.act(e, gcs, AF.Exp, r=[ki("gcs")], w=[ki("e")])
            self.tt("dve", dlt, ps3[0:64, 8:12], gcs, ALU.subtract, r=[pk3, ki("gcs")], w=[ki("dlt")])
            self.act(ft, dlt, AF.Exp, r=[ki("dlt")], w=[ki("ft")])
            self.cp("dve", bo["gl"][:], ps3[:, 8:12], r=[pk3], w=[ko("gl")])
            self.act(bo["gl"][:], bo["gl"][:], AF.Exp, r=[ko("gl")], w=[ko("gl")])
            self.stt(bo["nbe"][:], beta, -1.0, e, ALU.mult, ALU.mult, r=rG + [ki("e")], w=[ko("nbe")])
            self.tt("dve", bi["GB2"][:], g.unsqueeze(2).broadcast_to([64, 4, 64]), B_d.unsqueeze(1).broadcast_to([64, 4, 64]),
                    ALU.mult, r=rG, w=[ki("GB2")])
            ps4, pk4 = self.bank(d)
            self.mm(ps4[0:64, 0:256], A_d, bi["GB2"][:].rearrange("q h e -> q (h e)"), start=True, stop=False, r=rG + [ki("GB2")], w=[pk4])
            self.mm(ps4[0:64, 0:256], I64, NEG4[d][:].rearrange("q h e -> q (h e)"),
                    start=False, stop=True, r=rG + [("NEG4",)], w=[pk4])
            self.act(bi["DEC"][:].rearrange("q h e -> q (h e)"), ps4[0:64, 0:256], AF.Exp, r=[pk4], w=[ki("DEC")])
            ps5, pk5 = self.bank(d)
            ps6, pk6 = self.bank(d)
            for h in range(4):
                self.mm(ps5[0:64, h * 64:(h + 1) * 64], kT[:, h, :], kT[:, h, :], r=rQK, w=[pk5])
            for h in range(4):
                self.mm(ps6[0:64, h * 64:(h + 1) * 64], qT[:, h, :], kT[:, h, :], r=rQK, w=[pk6])
            yield
            if P4S <= 2:
                return
            self.tt("dve", bi["DG2"][:], A64([64, 4, 64]), e.unsqueeze(2).broadcast_to([64, 4, 64]), ALU.mult, r=rG + [ki("e")], w=[ki("DG2")])
            psE, pkE = self.bank(d)
            self.mm(psE[:, 0:256], self.ones_f[0:64, :], bi["DG2"][:].rearrange("q h e -> q (h e)"), r=[ki("DG2"), ("ones_f",)], w=[pkE])
            self.tt("dve", bo["qdT"][:], qT, psE[:, 0:256].rearrange("q (h e) -> q h e", e=64), ALU.mult, r=rQK + [pkE], w=[ko("qdT")])
            for h in range(4):
                self.act(bo["vb"][:, h, :], bi["KV"][:, 4 + h, :], AF.Copy, scale=beta[:, h:h + 1], r=rG + [ki("V")], w=[ko("vb")])
                self.act(bo["ktl"][:, h, :], bi["KV"][:, h, :], AF.Copy, scale=ft[:, h:h + 1], r=[ki("ft"), ki("K")], w=[ko("ktl")])
            self.cp("act", bo["kTc"][:], kT, r=rQK, w=[ko("kTc")])
            yield
            if P4S <= 3:
                return
            v4 = lambda ps_: ps_[0:64, 0:256].rearrange("q (h e) -> q h e", e=64)
            self.tt("dve", bi["QKD"][:], v4(ps6), bi["DEC"][:], ALU.mult, r=[pk6, ki("DEC")], w=[ki("QKD")])
            self.tt("dve", bi["DS"][:], bi["DEC"][:], NSTR.unsqueeze(1).broadcast_to([64, 4, 64]), ALU.mult, r=rG + [ki("DEC")], w=[ki("DS")])
            self.tt("dve", bi["DS"][:], v4(ps5), bi["DS"][:], ALU.mult, r=[pk5, ki("DS")], w=[ki("DS")])
            self.tt("dve", bi["M0"][:], bi["DS"][:], beta.unsqueeze(2).broadcast_to([64, 4, 64]), ALU.mult, r=rG + [ki("DS")], w=[ki("M0")])
            ps7, pk7 = self.bank(d)
            ps8, pk8 = self.bank(d)
            for h in range(4):
                self.tr(ps7[0:64, h * 64:(h + 1) * 64], bi["M0"][:, h, :], self.ident[0:64, 0:64], r=[ki("M0"), ("ident",)], w=[pk7])
            for h in range(4):
                self.tr(ps8[0:64, h * 64:(h + 1) * 64], bi["QKD"][:, h, :], self.ident[0:64, 0:64], r=[ki("QKD"), ("ident",)], w=[pk8])
            self.cp("act", bi["N0"][:], v4(ps7), r=[pk7], w=[ki("N0")])
            self.cp("dve", bo["qkT"][:], v4(ps8), r=[pk8], w=[ko("qkT")])
            yield
            if P4S <= 4:
                return
            M, NP = bi["M"], bi["NP"]
            psA, pkA = self.bank(d)
            psB, pkB = self.bank(d)
            for h in range(4):
                self.mm(psA[0:64, h * 64:(h + 1) * 64], bi["N0"][:, h, :], bi["M0"][:, h, :], r=[ki("N0"), ki("M0")], w=[pkA])
            for h in range(4):
                self.mm(psB[0:64, h * 64:(h + 1) * 64], bi["M0"][:, h, :], bi["N0"][:, h, :], r=[ki("N0"), ki("M0")], w=[pkB])
            self.cp("act", M[0][:], v4(psA), r=[pkA], w=[ki("M_0")])
            self.cp("dve", NP[0][:, :, 0:64], v4(psB), r=[pkB], w=[ki("NPn_0")])
            self.tt("dve", NP[0][:, :, 64:128], bi["N0"][:], A64([64, 4, 64]), ALU.add, r=rG + [ki("N0")], w=[ki("NPp_0")])
            yield
            for lev in range(1, 1 + int(os.environ.get('P4_LEV', '5'))):
                x, y = (lev - 1) % 2, lev % 2
                psY, pkY = self.bank(d)
                rx = [ki("M_%d" % x), ki("NPn_%d" % x), ki("NPp_%d" % x)]
                if lev < 5:
                    for h in range(4):
                        self.mm(psY[0:64, h * 128:(h + 1) * 128], M[x][:, h, :], NP[x][:, h, :], r=rx, w=[pkY])
                    psM, pkM = self.bank(d)
                    for h in range(4):
                        self.mm(psM[0:64, h * 64:(h + 1) * 64], NP[x][:, h, 0:64], M[x][:, h, :], r=rx, w=[pkM])
                else:
                    for h in range(4):
                        self.mm(psY[0:64, h * 128 + 64:(h + 1) * 128], M[x][:, h, :], NP[x][:, h, 64:128], r=rx, w=[pkY])
                Y3 = psY[0:64, :].rearrange("q (h e) -> q h e", e=128)
                if lev < 5:
                    self.tt("dve", NP[y][:, :, 64:128], Y3[:, :, 64:128], NP[x][:, :, 64:128], ALU.add, r=[pkY, ki("NPp_%d" % x)], w=[ki("NPp_%d" % y)])
                    self.cp("dve", NP[y][:, :, 0:64], Y3[:, :, 0:64], r=[pkY], w=[ki("NPn_%d" % y)])
                    self.cp("act", M[y][:], v4(psM), r=[pkM], w=[ki("M_%d" % y)])
                else:
                    self.tt("dve", bo["PT5"][:], Y3[:, :, 64:128], NP[x][:, :, 64:128], ALU.add, r=[pkY, ki("NPp_%d" % x)], w=[ko("PT5")])
                yield

        def scan(s):
            pb = s % 2
            ns = [order[d][s] for d in range(2)]
            ko = lambda d, nm: ("go", d, pb, nm)
            ki = lambda d, nm: ("gi", d, nm)
            bank = {}
            for d in range(2):
                bo = Bo[d][pb]
                bank[d] = self.bank("s")
                for h in range(4):
                    self.mm(bank[d][0][0:64, h * 128:(h + 1) * 128], bo["kTc"][:, h, :], S[d][:, h, :], r=[ko(d, "kTc"), ("S", d)], w=[bank[d][1]])
            yield
            for d in range(2):
                bo, bi = Bo[d][pb], Bi[d]
                ps_, pk_ = bank[d]
                self.tt("dve", bi["PRE"][:], ps_[0:64, :].rearrange("q (h e) -> q h e", e=128), bo["nbe"][:].unsqueeze(2).broadcast_to([64, 4, 128]),
                        ALU.mult, r=[pk_, ko(d, "nbe")], w=[ki(d, "PRE")])
                self.tt("dve", bi["PRE"][:], bi["PRE"][:], bo["vb"][:], ALU.add, r=[ki(d, "PRE"), ko(d, "vb")], w=[ki(d, "PRE")])
            yield
            for d in range(2):
                bo, bi = Bo[d][pb], Bi[d]
                bank[d] = self.bank("s")
                for h in range(4):
                    self.mm(bank[d][0][0:64, h * 128:(h + 1) * 128], bo["PT5"][:, h, :], bi["PRE"][:, h, :], r=[ko(d, "PT5"), ki(d, "PRE")], w=[bank[d][1]])
            yield
            for d in range(2):
                bi = Bi[d]
                self.cp("act", bi["VN"][:], bank[d][0][0:64, :].rearrange("q (h e) -> q h e", e=128), r=[bank[d][1]], w=[ki(d, "VN")])
            yield
            for d in range(2):
                bo, bi = Bo[d][pb], Bi[d]
                po, pko = self.bank("s")
                for h in range(4):
                    self.mm(po[:, h * 64:(h + 1) * 64], S[d][:, h, :], bo["qdT"][:, h, :], start=True, stop=False, r=[("S", d), ko(d, "qdT")], w=[pko])
                    self.mm(po[:, h * 64:(h + 1) * 64], bi["VN"][:, h, :], bo["qkT"][:, h, :], start=False, stop=True, r=[ki(d, "VN"), ko(d, "qkT")], w=[pko])
                pss, pks = self.bank("s")
                for h in range(4):
                    self.mm(pss[:, h * 128:(h + 1) * 128], bo["ktl"][:, h, :], bi["VN"][:, h, :], r=[ko(d, "ktl"), ki(d, "VN")], w=[pks])
                yield
                n = ns[d]
                oa = OA[:, :, n * 64:(n + 1) * 64]
                self.tt("dve", oa, po[:, 0:256].rearrange("q (h e) -> q h e", e=64), oa, ALU.add, r=[pko, ("OA", n)], w=[("OA", n)])
                for h in range(4):
                    self.stt(S[d][:, h, :], S[d][:, h, :], bo["gl"][:, h:h + 1], pss[:, h * 128:(h + 1) * 128], ALU.mult, ALU.add,
                             r=[("S", d), ko(d, "gl"), pks], w=[("S", d)])
            yield

        def run_gens(gens):
            gens = list(gens)
            while gens:
                for g_ in list(gens):
                    try:
                        next(g_)
                    except StopIteration:
                        gens.remove(g_)

        nsteps = NCH if self.upto != "p4s" else 6
        if P4S <= 0:
            p.barrier()
            return
        for d in range(2):
            ensure_tile(d, 0)
            ensure_tile(d, 1)
        run_gens([prepass(0, 0), prepass(1, 0)])
        if self.debug and l == 0 and P4S >= 99:
            for d in range(2):
                for nm, t_ in list(Bo[d][0].items()) + [(k_, Bi[d][k_]) for k_ in ("M0", "N0", "DEC", "KV", "QKD", "sm")]:
                    src_ = t_[:, 0:4, :] if nm == "sm" else t_[:]
                    dt_ = nc.dram_tensor("dbg_%d_%s" % (d, nm), list(src_.shape), F32, kind="ExternalOutput").ap()
                    p.dma("sp", dt_, src_, r=[("go", d, 0, nm), ("gi", d, nm), ("gi", d, "K"), ("gi", d, "V"), ("gi", d, "e"), ("gi", d, "ft")])
        for s in range(nsteps):
            gens = [scan(s)] if P4S >= 6 else []
            if s + 1 < nsteps:
                for d in range(2):
                    ensure_tile(d, s + 2)
                gens = [prepass(0, s + 1), prepass(1, s + 1)] + gens
            run_gens(gens)
        if self.debug and l == 0 and P4S >= 99:
            dt_ = nc.dram_tensor("dbg_OA", [128, 4, T], F32, kind="ExternalOutput").ap()
            p.dma("sp", dt_, OA[:], r=[("OA", n) for n in range(NCH)])
        p.barrier()
        ar.reset(m_scan)
        ZSt = [ar.alloc([128, 4, 512], BF16, "ZSt") for _ in range(2)]
        SQ = [ar.alloc([128, 512], BF16, "SQ") for _ in range(2)]
        LN = ar.alloc([128, 512], F32, "LN")
        RS = [ar.alloc([128, 512], F32, "RS") for _ in range(2)]
        Y1 = [ar.alloc([128, 512], F32, "Y1") for _ in range(2)]
        Yo = [ar.alloc([128, 4, 512], BF16, "Yo") for _ in range(2)]
        tiles = ([] if last else [(0, 256)]) + [(256 + 512 * i, 512) for i in range(8)]
        zv = self.zs.rearrange("(c q) t -> q c t", q=128)
        mv = self.mixT[512:1024, :].rearrange("(c q) t -> q c t", q=128)
        for ti, (t0, n) in enumerate(tiles):
            b = ti % 2
            p.dma("act", ZSt[b][:, :, 0:n], zv[:, :, t0:t0 + n], w=[("ZSt4", b)])
            rOA = [("OA", c) for c in range(t0 // 64, (t0 + n) // 64)]
            for h in range(4):
                hb = h % 2
                self.tt("dve", SQ[hb][:, 0:n], OA[:, h, t0:t0 + n], OA[:, h, t0:t0 + n], ALU.mult, r=rOA, w=[("SQ4", hb)])
                ps, pk = self.bank()
                self.mm(ps[:, 0:n], self.ones_bf[:], SQ[hb][:, 0:n], r=[("SQ4", hb), ("ones_bf",)], w=[pk])
                self.rsqrt_from_psum(RS[hb][:, 0:n], ps[:, 0:n], 128.0, n, r=[pk], w=[("RS4", hb)], tmp=LN[:, 0:n])
                self.stt(Y1[hb][:, 0:n], OA[:, h, t0:t0 + n], self.gngT[:, l:l + 1], RS[hb][:, 0:n], ALU.mult, ALU.mult,
                         r=rOA + [("RS4", hb), ("gngT",)], w=[("Y1", hb)])
                self.tt("dve", Yo[b][:, h, 0:n], Y1[hb][:, 0:n], ZSt[b][:, h, 0:n], ALU.mult, r=[("Y1", hb), ("ZSt4", b)], w=[("Yo", b, h)])
            p.dma("sp", mv[:, :, t0:t0 + n], Yo[b][:, :, 0:n], r=[("Yo", b, h) for h in range(4)], w=[("mixT4", ti)])
        p.barrier()

    def phase56(self, l, last):
        nc, p, ar, I = self.nc, self.p, self.arena, self.i
        ar.reset(self.m_phase)
        self.banks, self.pi = list(range(8)), 0
        Wo = ar.alloc([128, 8, DM], BF16, "Wo")
        W1 = ar.alloc([128, 8, 4 * DM], BF16, "W1")
        W2 = ar.alloc([128, 32, DM], BF16, "W2")
        mW = ar.mark()
        stg = [ar.alloc([128, 2048], F32, "stg") for _ in range(3)]
        self.wkeys, self.stg_i = {}, 0
        self.load_weight_bf16(Wo[:], I["w_out"][l].rearrange("(k q) f -> q k f", q=128), ("Wo",), stg)
        self.load_weight_bf16(W1[:], I["w_ff1"][l].rearrange("(k q) f -> q k f", q=128), ("W1",), stg)
        self.load_weight_bf16(W2[:], I["w_ff2"][l].rearrange("(k q) f -> q k f", q=128), ("W2",), stg)
        p.barrier()
        ar.reset(mW)
        NT = 256
        X = [ar.alloc([128, 8, NT], F32, "X") for _ in range(2)]
        MT = [ar.alloc([128, 8, NT], BF16, "MT") for _ in range(2)]
        HT2 = ar.alloc([128, 8, NT], BF16, "HT2")
        AT = ar.alloc([128, 32, NT], BF16, "AT")
        SQ = AT[:, 0:8, :]
        kSQ = [("AT", k) for k in range(8)]
        T1 = [ar.alloc([128, NT], F32, "T1") for _ in range(2)]
        RL = [ar.alloc([128, NT], F32, "RL") for _ in range(2)]
        LN = ar.alloc([128, NT], F32, "LN")
        RS = ar.alloc([128, NT], F32, "RS")
        xTv = self.xT.rearrange("(c q) t -> q c t", q=128)
        mTv = self.mixT.rearrange("(c q) t -> q c t", q=128)
        tiles = [(NT * i, NT, 0 if i == 0 else 1) for i in range(T // NT)]
        if last:
            tiles = tiles[1:]
        n = NT
        for ti, (t0, n_, col) in enumerate(tiles):
            b = ti % 2
            Xt = X[b]
            kX = [("X", b, k) for k in range(8)]
            p.dma("sp", Xt[:], xTv[:, :, t0:t0 + n], w=kX)
            p.dma("act", MT[b][:], mTv[:, :, t0:t0 + n], w=[("MT", b)])
            for dc in range(8):
                ps, pk = self.bank()
                for k in range(8):
                    self.mm(ps[:, 0:n], Wo[:, k, dc * 128:(dc + 1) * 128], MT[b][:, k, :], start=(k == 0), stop=(k == 7),
                            r=[("Wo",), ("MT", b)], w=[pk])
                self.stt(Xt[:, dc, :], ps[:, 0:n], self.modv(l, 2, dc, col), Xt[:, dc, :], ALU.mult, ALU.add,
                         r=[pk, ("X", b, dc), ("modT", l)], w=[("X", b, dc)])
            self.tt("dve", SQ, Xt[:], Xt[:], ALU.mult, r=kX, w=kSQ)
            ps, pk = self.bank()
            for k in range(8):
                self.mm(ps[:, 0:n], self.ones_bf[:], SQ[:, k, :], start=(k == 0), stop=(k == 7), r=kSQ + [("ones_bf",)], w=[pk])
            self.rsqrt_from_psum(RS[:], ps[:, 0:n], float(DM), n, r=[pk], w=[("RS5",)], tmp=LN[:])
            for k in range(8):
                tb = k % 2
                self.stt(T1[tb][:], Xt[:, k, :], self.A2[:, l, k, col:col + 1], RS[:], ALU.mult, ALU.mult,
                         r=[("X", b, k), ("RS5",), ("A", l, 32)], w=[("T15", tb)])
                self.act(HT2[:, k, :], T1[tb][:], AF.Identity, bias=self.modv(l, 3, k, col), r=[("T15", tb), ("modT", l)], w=[("HT2", k)])
            rH = [("HT2", k) for k in range(8)]
            for fc in range(32):
                ps, pk = self.bank()
                for k in range(8):
                    self.mm(ps[:, 0:n], W1[:, k, fc * 128:(fc + 1) * 128], HT2[:, k, :], start=(k == 0), stop=(k == 7), r=rH + [("W1",)], w=[pk])
                rb = fc % 2
                self.act(RL[rb][:], ps[:, 0:n], AF.Relu, r=[pk], w=[("RL", rb)])
                self.tt("dve", AT[:, fc, :], RL[rb][:], RL[rb][:], ALU.mult, r=[("RL", rb)], w=[("AT", fc)])
            rA = [("AT", fc) for fc in range(32)]
            for dc in range(8):
                ps, pk = self.bank()
                for fc in range(32):
                    self.mm(ps[:, 0:n], W2[:, fc, dc * 128:(dc + 1) * 128], AT[:, fc, :], start=(fc == 0), stop=(fc == 31), r=rA + [("W2",)], w=[pk])
                self.stt(Xt[:, dc, :], ps[:, 0:n], self.modv(l, 5, dc, col), Xt[:, dc, :], ALU.mult, ALU.add,
                         r=[pk, ("X", b, dc), ("modT", l)], w=[("X", b, dc)])
            p.dma("sp", xTv[:, :, t0:t0 + n], Xt[:], r=kX, w=[("xTo", ti)])
        p.barrier()

    def epilogue(self):
        nc, p, ar, I = self.nc, self.p, self.arena, self.i
        ar.reset(self.m_phase)
        self.banks, self.pi = list(range(8)), 0
        X = [ar.alloc([128, 8, 512], F32, "X") for _ in range(2)]
        SQ = ar.alloc([128, 8, 512], BF16, "SQ")
        Y = ar.alloc([128, 8, 512], F32, "Y")
        LN = ar.alloc([128, 512], F32, "LN")
        RS = ar.alloc([128, 512], F32, "RS")
        OUT = [ar.alloc([128, DM], F32, "OUT") for _ in range(2)]
        xTv = self.xT.rearrange("(c q) t -> q c t", q=128)
        n = 512
        for ti in range(T_L // 512):
            t0 = T_C + ti * 512
            b = ti % 2
            p.dma("sp", X[b][:], xTv[:, :, t0:t0 + n], w=[("X", b)])
            self.tt("dve", SQ[:], X[b][:], X[b][:], ALU.mult, r=[("X", b)], w=[("SQ",)])
            ps, pk = self.bank()
            for k in range(8):
                self.mm(ps[:, 0:n], self.ones_bf[:], SQ[:, k, :], start=(k == 0), stop=(k == 7), r=[("SQ",), ("ones_bf",)], w=[pk])
            self.rsqrt_from_psum(RS[:], ps[:, 0:n], float(DM), n, r=[pk], w=[("RSe",)], tmp=LN[:])
            for k in range(8):
                self.stt(Y[:, k, :], X[b][:, k, :], self.fngT[:, k:k + 1], RS[:], ALU.mult, ALU.mult, r=[("X", b), ("RSe",), ("fngT",)], w=[("Y", k)])
            for j in range(4):
                ob = (ti * 4 + j) % 2
                for hb in range(2):
                    ps, pk = self.bank()
                    for c in range(4):
                        k = hb * 4 + c
                        self.tr(ps[:, c * 128:(c + 1) * 128], Y[:, k, j * 128:(j + 1) * 128], self.ident[:], r=[("Y", k), ("ident",)], w=[pk])
                    self.cp("act" if hb == 0 else "dve", OUT[ob][:, hb * 512:(hb + 1) * 512], ps[:, :], r=[pk], w=[("OUT", ob, hb)])
                r0 = ti * 512 + j * 128
                p.dma("sp", self.out[r0:r0 + 128, :], OUT[ob][:], r=[("OUT", ob, 0), ("OUT", ob, 1)], w=[("out", ti, j)])


def _consts():
    ident = np.eye(128, dtype=np.float32)
    t = np.arange(T_L)
    row = (t // 64).astype(np.float32)
    col = (t % 64).astype(np.float32)
    inv_freq = (np.float32(10000.0) ** (-np.arange(8, dtype=np.float32) / np.float32(8))).astype(np.float32)
    ang = np.concatenate([row[:, None] * inv_freq, col[:, None] * inv_freq], axis=-1).astype(np.float32)
    cos = np.cos(ang).astype(np.float32).T
    sin = np.sin(ang).astype(np.float32).T
    ropec = np.ones((32, T), np.float32)
    ropes = np.zeros((32, T), np.float32)
    ropec[0:16, T_C:] = cos
    ropec[16:32, T_C:] = cos
    ropes[0:16, T_C:] = sin
    ropes[16:32, T_C:] = sin
    ii = np.arange(64)
    gm = np.zeros((64, 10, 64), np.float32)
    A0 = ii[:, None] <= ii[None, :]
    A1 = ii[:, None] >= ii[None, :]
    B0 = ii[:, None] > ii[None, :]
    B1 = ii[:, None] < ii[None, :]
    gm[:, 0], gm[:, 1], gm[:, 2], gm[:, 3] = A0, A1, B0, B1
    gm[:, 4] = np.where(A0.T, 0.0, -30000.0)
    gm[:, 5] = np.where(A1.T, 0.0, -30000.0)
    gm[:, 6] = -B0.astype(np.float32)
    gm[:, 7] = -B1.astype(np.float32)
    gm[:, 8] = np.eye(64, dtype=np.float32)
    return dict(c_ident=ident, c_ropec=ropec, c_ropes=ropes, c_gmask=gm)


def _in_map(inputs, b, consts):
    f = lambda a: np.ascontiguousarray(np.asarray(a, dtype=np.float32))
    m = dict(
        x=f(inputs["x"][b]), ctx=f(inputs["ctx"][b]),
        cc=f(np.stack([np.asarray(inputs["c_ctx"]), np.asarray(inputs["c"][b])], 0)),
        a_log=f(np.asarray(inputs["a_log"]).reshape(L, 8)), dt_bias=f(np.asarray(inputs["dt_bias"]).reshape(L, 8)),
        final_norm_g=f(np.asarray(inputs["final_norm_g"]).reshape(1, DM)),
    )
    for k in ("w_ada", "b_ada", "norm1_g", "w_in", "q_a_g", "w_q_b", "kv_a_g", "w_kv_b", "conv_w", "gdn_norm_g",
              "w_out", "norm2_g", "w_ff1", "w_ff2"):
        m[k] = f(inputs[k])
    m.update(consts)
    return m


def kernel(**inputs):
    nc = Builder().build()
    consts = _consts()
    in_maps = [_in_map(inputs, b, consts) for b in range(8)]
    res = run_bass_kernel_spmd(nc, in_maps, core_ids=list(range(8)))
    return np.stack([np.asarray(r["out"], dtype=np.float32) for r in res.results], 0)
```
